# Optimizing a Trainium2 kernel written in Bass

```python
import math, functools
import jax, jax.numpy as jnp
from jax import lax
import numpy as np

D_MODEL = 2048
BATCH = 32
SEQ = 256
DEPTH = 4
DEC_BATCH = 4
DEC_SEQ = 4096
PAST_LEN = 512

GRID_W = 64
N_BRANCH = 4
BRANCH_W = 512
MLA_HEADS = 8
MLA_NOPE = 64
MLA_ROPE = 32
MLA_V = 64
MLA_Q_LORA = 512
MLA_KV_LORA = 256
NA_HEADS = 8
NA_HD = 64
NA_WIN_R = 8
NA_WIN_C = 16
POOL_WINDOWS = (2, 4, 8, 16)
N_POOL_GROUPS = 4
POOL_GROUP = 128
POOL_W = N_POOL_GROUPS * POOL_GROUP
DIFF_HEADS = 4
DIFF_HD = 64
D_FF = 4 * D_MODEL
Q_BLOCK = 128
ROPE_THETA = 10000.0
RMS_EPS = 1e-6
NEG_INF = -1e30
NA_W = NA_HEADS * NA_HD
DIFF_QK_W = DIFF_HEADS * 2 * DIFF_HD
DIFF_V_W = DIFF_HEADS * 2 * DIFF_HD
MLA_SCALE = (MLA_NOPE + MLA_ROPE) ** -0.5
IN_SIZES = (MLA_Q_LORA, MLA_KV_LORA, MLA_ROPE, NA_W, NA_W, NA_W, POOL_W, DIFF_QK_W, DIFF_QK_W, DIFF_V_W, N_BRANCH * D_MODEL)
IN_COLS = sum(IN_SIZES)
IN_SPLITS = tuple(sum(IN_SIZES[:i + 1]) for i in range(len(IN_SIZES) - 1))

kernel_name = 'hybrid_flow_trunk_step'


def rms_norm(x, g):
    xf = x.astype(jnp.float32)
    y = xf * lax.rsqrt(jnp.mean(jnp.square(xf), axis=-1, keepdims=True) + RMS_EPS)
    return (y * g.astype(jnp.float32)).astype(x.dtype)


def split_heads(x, n):
    return x.reshape(*x.shape[:-1], n, x.shape[-1] // n)


def axial_rope(n_tokens, rot_dim):
    t = jnp.arange(n_tokens)
    row = (t // GRID_W).astype(jnp.float32)
    col = (t % GRID_W).astype(jnp.float32)
    n_freq = rot_dim // 4
    inv = ROPE_THETA ** (-jnp.arange(n_freq, dtype=jnp.float32) / n_freq)
    ang = jnp.concatenate([row[:, None] * inv, col[:, None] * inv], axis=-1)
    return jnp.cos(ang), jnp.sin(ang)


def apply_rope(x, cos, sin):
    xf = x.astype(jnp.float32)
    x1, x2 = jnp.split(xf, 2, axis=-1)
    c = cos[:, None, :]
    s = sin[:, None, :]
    return jnp.concatenate([x1 * c - x2 * s, x2 * c + x1 * s], axis=-1).astype(x.dtype)


def _query_blocks(x):
    b, t = x.shape[:2]
    return jnp.moveaxis(x.reshape(b, t // Q_BLOCK, Q_BLOCK, *x.shape[2:]), 1, 0)


def _merge_blocks(y):
    y = jnp.moveaxis(y, 0, 1)
    return y.reshape(y.shape[0], -1, *y.shape[3:])


def _probs(q, k, scale):
    s = jnp.einsum('bqhd,bkhd->bhqk', q, k, preferred_element_type=jnp.float32) * scale
    return jax.nn.softmax(s, axis=-1)


def attend_blocked(q, k, v, scale):
    def one_block(qb):
        p = _probs(qb, k, scale).astype(v.dtype)
        return jnp.einsum('bhqk,bkhd->bqhd', p, v)
    return _merge_blocks(lax.map(one_block, _query_blocks(q)))


def mla_compress(q_c, kv_c, lp):
    q = rms_norm(q_c, lp['g_q_lora']) @ lp['w_uq']
    return split_heads(q, MLA_HEADS), rms_norm(kv_c, lp['g_kv_lora'])


def mla_expand(ckv, k_r, lp):
    k_nope, v = jnp.split(split_heads(ckv @ lp['w_ukv'], MLA_HEADS), [MLA_NOPE], axis=-1)
    k_rope = jnp.broadcast_to(k_r[:, :, None, :], k_nope.shape[:-1] + (MLA_ROPE,))
    return jnp.concatenate([k_nope, k_rope], axis=-1), v


def diff_attention(q1, q2, k1, k2, v, lam_p, norm_g, li):
    b, t = q1.shape[:2]
    lam_init = 0.8 - 0.6 * math.exp(-0.3 * li)
    lp32 = lam_p.astype(jnp.float32)
    lam = jnp.exp(jnp.sum(lp32[0] * lp32[1])) - jnp.exp(jnp.sum(lp32[2] * lp32[3])) + lam_init
    scale = DIFF_HD ** -0.5
    def one_block(qs):
        a, c2 = qs
        p = _probs(a, k1, scale) - lam * _probs(c2, k2, scale)
        return jnp.einsum('bhqk,bkhd->bqhd', p.astype(v.dtype), v)
    o = _merge_blocks(lax.map(one_block, (_query_blocks(q1), _query_blocks(q2))))
    o = rms_norm(o, norm_g) * (1.0 - lam_init)
    return o.reshape(b, t, -1)


def neighbourhood_attention(q, k, v, k_ctx, v_ctx, rpb):
    b, t, h, dh = q.shape
    rows = t // GRID_W
    wr = min(NA_WIN_R, rows)
    ncb = GRID_W // NA_WIN_C
    gw = 2 * NA_WIN_C
    r = jnp.arange(rows)
    row_idx = jnp.clip(r - wr // 2, 0, rows - wr)[:, None] + jnp.arange(wr)
    j = jnp.arange(ncb)
    col_idx = jnp.clip(j * NA_WIN_C - NA_WIN_C // 2, 0, GRID_W - gw)[:, None] + jnp.arange(gw)
    qcol = jnp.arange(GRID_W).reshape(ncb, NA_WIN_C)
    col_start = jnp.clip(qcol - NA_WIN_C // 2, 0, GRID_W - NA_WIN_C)
    kc = col_idx[:, None, :]
    valid = (kc >= col_start[..., None]) & (kc < col_start[..., None] + NA_WIN_C)
    dr = row_idx - r[:, None] + NA_WIN_R - 1
    dc = jnp.clip(kc - qcol[..., None] + NA_WIN_C - 1, 0, 2 * NA_WIN_C - 2)
    bias = rpb[:, dr[:, None, None, :, None], dc[None, :, :, None, :]].astype(jnp.float32)
    bias = jnp.where(valid[None, None, :, :, None, :], bias, NEG_INF)
    bias = bias.reshape(h, rows, ncb, NA_WIN_C, wr * gw).transpose(1, 2, 0, 3, 4)
    gidx = (row_idx[:, None, :, None], col_idx[None, :, None, :])
    kg = k.reshape(b, rows, GRID_W, h, dh)[:, gidx[0], gidx[1]].reshape(b, rows, ncb, wr * gw, h, dh)
    vg = v.reshape(b, rows, GRID_W, h, dh)[:, gidx[0], gidx[1]].reshape(b, rows, ncb, wr * gw, h, dh)
    qg = q.reshape(b, rows, ncb, NA_WIN_C, h, dh)
    scale = dh ** -0.5
    s_win = jnp.einsum('brjqhd,brjkhd->brjhqk', qg, kg, preferred_element_type=jnp.float32) * scale + bias[None]
    s_ctx = jnp.einsum('brjqhd,blhd->brjhql', qg, k_ctx, preferred_element_type=jnp.float32) * scale
    p = jax.nn.softmax(jnp.concatenate([s_win, s_ctx], axis=-1), axis=-1).astype(v.dtype)
    nw = wr * gw
    o = (jnp.einsum('brjhqk,brjkhd->brjqhd', p[..., :nw], vg)
         + jnp.einsum('brjhql,blhd->brjqhd', p[..., nw:], v_ctx))
    return o.reshape(b, t, h * dh)


def multiscale_pool(u, pool_w, pool_scale):
    b, t, _ = u.shape
    uf = u.astype(jnp.float32)
    csum = jnp.concatenate([jnp.zeros((b, 1, POOL_W), jnp.float32), jnp.cumsum(uf, axis=1)], axis=1)
    pos = jnp.arange(t)
    groups = []
    for gi, w in enumerate(POOL_WINDOWS):
        sl = slice(gi * POOL_GROUP, (gi + 1) * POOL_GROUP)
        lo = jnp.clip(pos - w // 2, 0, t)
        hi = jnp.clip(pos + w // 2, 0, t)
        mean = (csum[:, hi, sl] - csum[:, lo, sl]) / (hi - lo).astype(jnp.float32)[None, :, None]
        groups.append(mean - uf[:, :, sl])
    pooled = jnp.stack(groups, axis=2).astype(u.dtype)
    mixed = jnp.einsum('btgc,gcd->btgd', pooled, pool_w).reshape(b, t, POOL_W)
    return mixed * pool_scale


def merge_branches(outs, gates, w_br, w_o):
    g = jax.nn.sigmoid(gates.astype(jnp.float32)).astype(gates.dtype)
    gs = jnp.split(g, N_BRANCH, axis=-1)
    merged = gs[0] * (outs[0] @ w_br[0])
    for bi in range(1, N_BRANCH):
        merged = merged + gs[bi] * (outs[bi] @ w_br[bi])
    return merged @ w_o


def context_mixers(h, lp, li):
    b, l, _ = h.shape
    (q_c, kv_c, k_r, na_q, na_k, na_v, pool_in, dq, dk, dv, gates) = jnp.split(h @ lp['w_in'], IN_SPLITS, axis=-1)
    q, ckv = mla_compress(q_c, kv_c, lp)
    k_a, v_a = mla_expand(ckv, k_r, lp)
    o_a = attend_blocked(q, k_a, v_a, MLA_SCALE).reshape(b, l, -1)
    na_k = split_heads(na_k, NA_HEADS)
    na_v = split_heads(na_v, NA_HEADS)
    o_b = attend_blocked(split_heads(na_q, NA_HEADS), na_k, na_v, NA_HD ** -0.5).reshape(b, l, -1)
    o_c = multiscale_pool(pool_in, lp['pool_w'], lp['pool_scale'])
    dq = split_heads(dq, DIFF_HEADS)
    dk = split_heads(dk, DIFF_HEADS)
    dv = split_heads(dv, DIFF_HEADS)
    o_d = diff_attention(dq[..., :DIFF_HD], dq[..., DIFF_HD:], dk[..., :DIFF_HD], dk[..., DIFF_HD:], dv,
                         lp['diff_lambda'], lp['diff_norm_g'], li)
    y = merge_branches((o_a, o_b, o_c, o_d), gates, lp['w_br'], lp['w_o'])
    return y, (ckv, k_r, na_k, na_v, dk, dv)


def latent_mixers(h, lp, li, ctx, rope_mla, rope_diff):
    ckv_c, kr_c, nak_c, nav_c, dk_c, dv_c = ctx
    b, t, _ = h.shape
    (q_c, kv_c, k_r, na_q, na_k, na_v, pool_in, dq, dk, dv, gates) = jnp.split(h @ lp['w_in'], IN_SPLITS, axis=-1)
    q, ckv = mla_compress(q_c, kv_c, lp)
    q = jnp.concatenate([q[..., :MLA_NOPE], apply_rope(q[..., MLA_NOPE:], *rope_mla)], axis=-1)
    k_r = apply_rope(k_r[:, :, None, :], *rope_mla)[:, :, 0, :]
    k_lat, v_lat = mla_expand(ckv, k_r, lp)
    k_ctx, v_ctx = mla_expand(ckv_c, kr_c, lp)
    o_a = attend_blocked(q, jnp.concatenate([k_lat, k_ctx], axis=1), jnp.concatenate([v_lat, v_ctx], axis=1),
                         MLA_SCALE).reshape(b, t, -1)
    o_b = neighbourhood_attention(split_heads(na_q, NA_HEADS), split_heads(na_k, NA_HEADS),
                                  split_heads(na_v, NA_HEADS), nak_c, nav_c, lp['na_rpb'])
    o_c = multiscale_pool(pool_in, lp['pool_w'], lp['pool_scale'])
    dq = split_heads(dq, DIFF_HEADS)
    dk = split_heads(dk, DIFF_HEADS)
    q1 = apply_rope(dq[..., :DIFF_HD], *rope_diff)
    q2 = apply_rope(dq[..., DIFF_HD:], *rope_diff)
    k1 = jnp.concatenate([apply_rope(dk[..., :DIFF_HD], *rope_diff), dk_c[..., :DIFF_HD]], axis=1)
    k2 = jnp.concatenate([apply_rope(dk[..., DIFF_HD:], *rope_diff), dk_c[..., DIFF_HD:]], axis=1)
    vd = jnp.concatenate([split_heads(dv, DIFF_HEADS), dv_c], axis=1)
    o_d = diff_attention(q1, q2, k1, k2, vd, lp['diff_lambda'], lp['diff_norm_g'], li)
    y = merge_branches((o_a, o_b, o_c, o_d), gates, lp['w_br'], lp['w_o'])
    return y, None


def trunk_layer(x, cond, lp, mixer):
    mod = jax.nn.silu(cond) @ lp['w_mod'] + lp['b_mod']
    sh1, sc1, g1, sh2, sc2, g2 = jnp.split(mod[:, None, :], 6, axis=-1)
    gn = lp['g_norm']
    h = rms_norm(x, gn[0]) * (1.0 + sc1) + sh1
    y, extra = mixer(h)
    x = x + g1 * rms_norm(y, gn[1])
    h = rms_norm(x, gn[2]) * (1.0 + sc2) + sh2
    y = jnp.square(jax.nn.relu(h @ lp['w_up'])) @ lp['w_down']
    x = x + g2 * rms_norm(y, gn[3])
    return x, extra


def setup_inputs(seed: int = 0) -> dict:
    key = jax.random.key(seed)
    ks = jax.random.split(key, 32)
    def nrm(i, shape, scale=1.0):
        return jax.random.normal(ks[i], shape, jnp.float32) * scale
    return {
        'x_prompt': nrm(0, (BATCH, SEQ, D_MODEL)),
        'x_sample': nrm(1, (DEC_BATCH, DEC_SEQ, D_MODEL)),
        'cache_mla_ckv': nrm(2, (DEC_BATCH, DEPTH, PAST_LEN, MLA_KV_LORA)),
        'cache_mla_krope': nrm(3, (DEC_BATCH, DEPTH, PAST_LEN, MLA_ROPE)),
        'cache_na_k': nrm(4, (DEC_BATCH, DEPTH, PAST_LEN, NA_HEADS, NA_HD)),
        'cache_na_v': nrm(5, (DEC_BATCH, DEPTH, PAST_LEN, NA_HEADS, NA_HD)),
        'cache_diff_k': nrm(6, (DEC_BATCH, DEPTH, PAST_LEN, DIFF_HEADS, 2 * DIFF_HD)),
        'cache_diff_v': nrm(7, (DEC_BATCH, DEPTH, PAST_LEN, DIFF_HEADS, 2 * DIFF_HD)),
        'c': nrm(8, (DEC_BATCH, D_MODEL)),
        'c_ctx': nrm(9, (D_MODEL,)),
        'w_mod': nrm(10, (DEPTH, D_MODEL, 6 * D_MODEL), 0.5 * D_MODEL ** -0.5),
        'b_mod': nrm(11, (DEPTH, 6 * D_MODEL), 0.01),
        'g_norm': 1.0 + nrm(12, (DEPTH, 4, D_MODEL), 0.02),
        'w_in': nrm(13, (DEPTH, D_MODEL, IN_COLS), D_MODEL ** -0.5),
        'g_q_lora': 1.0 + nrm(14, (DEPTH, MLA_Q_LORA), 0.02),
        'g_kv_lora': 1.0 + nrm(15, (DEPTH, MLA_KV_LORA), 0.02),
        'w_uq': nrm(16, (DEPTH, MLA_Q_LORA, MLA_HEADS * (MLA_NOPE + MLA_ROPE)), MLA_Q_LORA ** -0.5),
        'w_ukv': nrm(17, (DEPTH, MLA_KV_LORA, MLA_HEADS * (MLA_NOPE + MLA_V)), MLA_KV_LORA ** -0.5),
        'na_rpb': nrm(18, (DEPTH, NA_HEADS, 2 * NA_WIN_R - 1, 2 * NA_WIN_C - 1), 0.1),
        'pool_w': nrm(19, (DEPTH, N_POOL_GROUPS, POOL_GROUP, POOL_GROUP), POOL_GROUP ** -0.5),
        'pool_scale': 1.0 + nrm(20, (DEPTH, POOL_W), 0.1),
        'diff_lambda': nrm(21, (DEPTH, 4, DIFF_HD), 0.1),
        'diff_norm_g': 1.0 + nrm(22, (DEPTH, 2 * DIFF_HD), 0.02),
        'w_br': nrm(23, (DEPTH, N_BRANCH, BRANCH_W, D_MODEL), BRANCH_W ** -0.5),
        'w_o': nrm(24, (DEPTH, D_MODEL, D_MODEL), D_MODEL ** -0.5),
        'w_up': nrm(25, (DEPTH, D_MODEL, D_FF), D_MODEL ** -0.5),
        'w_down': nrm(26, (DEPTH, D_FF, D_MODEL), D_FF ** -0.5),
    }


def reference(x_prompt, x_sample, cache_mla_ckv, cache_mla_krope, cache_na_k, cache_na_v, cache_diff_k,
              cache_diff_v, c, c_ctx, w_mod, b_mod, g_norm, w_in, g_q_lora, g_kv_lora, w_uq, w_ukv, na_rpb,
              pool_w, pool_scale, diff_lambda, diff_norm_g, w_br, w_o, w_up, w_down):
    params = dict(w_mod=w_mod, b_mod=b_mod, g_norm=g_norm, w_in=w_in, g_q_lora=g_q_lora, g_kv_lora=g_kv_lora,
                  w_uq=w_uq, w_ukv=w_ukv, na_rpb=na_rpb, pool_w=pool_w, pool_scale=pool_scale,
                  diff_lambda=diff_lambda, diff_norm_g=diff_norm_g, w_br=w_br, w_o=w_o, w_up=w_up, w_down=w_down)
    xp = x_prompt
    cond_ctx = c_ctx[None, :]
    ctx_states = []
    for i in range(DEPTH):
        lp = {name: arr[i] for name, arr in params.items()}
        xp, ctx_i = trunk_layer(xp, cond_ctx, lp, functools.partial(context_mixers, lp=lp, li=i))
        ctx_states.append(ctx_i)
    state_mla_ckv = jnp.stack([s[0] for s in ctx_states], axis=1)
    state_mla_krope = jnp.stack([s[1] for s in ctx_states], axis=1)
    state_na_k = jnp.stack([s[2] for s in ctx_states], axis=1)
    state_na_v = jnp.stack([s[3] for s in ctx_states], axis=1)
    state_diff_k = jnp.stack([s[4] for s in ctx_states], axis=1)
    state_diff_v = jnp.stack([s[5] for s in ctx_states], axis=1)
    n_lat = x_sample.shape[1]
    rope_mla = axial_rope(n_lat, MLA_ROPE)
    rope_diff = axial_rope(n_lat, DIFF_HD)
    xs = x_sample
    for i in range(DEPTH):
        lp = {name: arr[i] for name, arr in params.items()}
        ctx_i = (cache_mla_ckv[:, i], cache_mla_krope[:, i], cache_na_k[:, i], cache_na_v[:, i],
                 cache_diff_k[:, i], cache_diff_v[:, i])
        xs, _ = trunk_layer(xs, c, lp, functools.partial(latent_mixers, lp=lp, li=i, ctx=ctx_i,
                                                         rope_mla=rope_mla, rope_diff=rope_diff))
    return (xp, xs, state_mla_ckv, state_mla_krope, state_na_k, state_na_v, state_diff_k, state_diff_v)
```

```python
import math
import numpy as np
import ml_dtypes
import concourse.bass as bass
import concourse.mybir as mybir
from concourse.bass_utils import run_bass_kernel_spmd

F32 = mybir.dt.float32
BF16 = mybir.dt.bfloat16
AF = mybir.ActivationFunctionType
ALU = mybir.AluOpType
AX = mybir.AxisListType

D = 2048
KC_D = 16
IN_COLS = 12576
DFF = 8192
RMS_EPS = 1e-6
GRID_W = 64
PAST = 512
SEQ_P = 256


class Eng:
    def __init__(self, P, name, h):
        self.P = P
        self.name = name
        self.h = h
        self.sem = P.raw_sem("e_" + name)
        self.count = 0
        self.seen = {}

    def wait(self, sem, val):
        key = id(sem)
        if self.seen.get(key, 0) >= val:
            return
        self.seen[key] = val
        self.h.wait_ge(sem, val)
        self.P.n_wait += 1

    def need(self, sem, val):
        key = id(sem)
        if self.seen.get(key, 0) >= val:
            return False
        self.seen[key] = val
        return True


class Tile:
    def __init__(self, P, name, ap=None):
        self.P = P
        self.name = name
        self.ap = ap
        self.w = None
        self.r = []
        self.ds = {}
        self.excl = False
        P.tiles.append(self)

    def __getitem__(self, idx):
        return self.ap[idx]


class Prog:
    def __init__(self, nc, debug=False):
        self.nc = nc
        self.debug = debug
        self.n_wait = 0
        self.n_ins = 0
        self.attach = True
        self._sems = []
        self._stack = []
        self.tiles = []
        self.free_dsems = {"sw": [], "hw": []}
        self.all_dsems = []
        self.pe = Eng(self, "pe", nc.tensor)
        self.act = Eng(self, "act", nc.scalar)
        self.dve = Eng(self, "dve", nc.vector)
        self.pool = Eng(self, "pool", nc.gpsimd)
        self.sp = Eng(self, "sp", nc.sync)
        self.engs = [self.pe, self.act, self.dve, self.pool, self.sp]
        self.rr = 0

    def raw_sem(self, name):
        cm = self.nc.semaphore(name + "_%d" % len(self._sems))
        s = cm.__enter__()
        self._sems.append(cm)
        return s

    def get_dsem(self, kind):
        if self.free_dsems[kind]:
            return self.free_dsems[kind].pop()
        d = [self.raw_sem("d" + kind), 0]
        self.all_dsems.append(d)
        return d

    def sbuf(self, name, shape, dtype):
        self.uid = getattr(self, "uid", 0) + 1
        cm = self.nc.sbuf_tensor(name + "_%d" % self.uid, list(shape), dtype)
        t = cm.__enter__()
        self._stack.append(cm)
        return Tile(self, name, t)

    def psum(self, name, shape, dtype=F32):
        self.uid = getattr(self, "uid", 0) + 1
        cm = self.nc.psum_tensor(name + "_%d" % self.uid, list(shape), dtype)
        t = cm.__enter__()
        self._stack.append(cm)
        tl = Tile(self, name, t)
        tl.excl = True
        return tl

    def dram(self, name, shape, dtype):
        kind = "ExternalOutput" if self.debug else "Internal"
        return self.nc.dram_tensor(name, list(shape), dtype, kind=kind).ap()

    def mark(self):
        return (len(self._stack), len(self.tiles))

    def release(self, mark):
        ns, nt = mark
        for t in self.tiles[nt:]:
            for (dr_, kind), d in t.ds.items():
                if d[1] < 1500:
                    self.free_dsems[kind].append(d)
        del self.tiles[nt:]
        while len(self._stack) > ns:
            self._stack.pop().__exit__(None, None, None)

    def _deps(self, eng, reads, writes, acc_ok=False):
        need = {}

        def add(s, v):
            k = id(s)
            if k not in need or need[k][1] < v:
                need[k] = (s, v)
        for t in reads:
            if t.w is not None:
                add(t.w[0], t.w[1])
            if t.excl:
                for (s, v) in t.r:
                    add(s, v)
        for t in writes:
            if t.w is not None:
                if not (acc_ok and t.w[0] is eng.sem):
                    add(t.w[0], t.w[1])
            for (s, v) in t.r:
                add(s, v)
        todo = [(s, v) for (s, v) in need.values() if eng.need(s, v)]
        if not self.attach:
            for (s, v) in todo:
                eng.h.wait_ge(s, v)
                self.n_wait += 1
            return None
        for (s, v) in todo[:-1]:
            eng.h.wait_ge(s, v)
            self.n_wait += 1
        return todo[-1] if todo else None

    def _record(self, ticket, reads, writes):
        for t in reads:
            t.r.append(ticket)
            if len(t.r) > 48:
                best = {}
                for (s, v) in t.r:
                    k = id(s)
                    if k not in best or best[k][1] < v:
                        best[k] = (s, v)
                t.r = list(best.values())
        for t in writes:
            t.w = ticket
            t.r = []

    def _roll(self, eng):
        if eng.count >= 30000:
            eng.prev = (eng.sem, eng.count)
            eng.sem = self.raw_sem("e_" + eng.name)
            eng.count = 0

    def op(self, eng, ins_fn, reads=(), writes=()):
        self._roll(eng)
        w = self._deps(eng, reads, writes)
        ins = ins_fn(eng.h)
        if w is not None:
            ins._wait_ge(w[0], w[1])
        eng.count += 1
        ins.then_inc(eng.sem, 1)
        self._record((eng.sem, eng.count), reads, writes)
        self.n_ins += 1
        return ins

    def mm(self, fns, reads=(), writes=(), acc=False):
        eng = self.pe
        self._roll(eng)
        w = self._deps(eng, reads, writes, acc_ok=acc)
        ins = None
        for f in fns:
            ins = f(eng.h)
            if w is not None:
                ins._wait_ge(w[0], w[1])
                w = None
            self.n_ins += 1
        eng.count += 1
        ins.then_inc(eng.sem, 1)
        self._record((eng.sem, eng.count), reads, writes)

    def dma(self, q, out_ap, in_ap, reads=(), writes=(), **kw):
        nd = 1
        for ap_ in (out_ap, in_ap):
            sh = list(ap_.shape)
            tot = 1
            for x_ in sh:
                tot *= x_
            nd = max(nd, tot // max(1, sh[-1]))
        self.max_desc = max(getattr(self, "max_desc", 0), nd)
        assert nd <= 2048 or kw.get("allow_slow_non_contiguous"), ("too many descriptors", nd, out_ap.shape, in_ap.shape)
        w = self._deps(q, reads, writes)
        ins = q.h.dma_start(out=out_ap, in_=in_ap, **kw)
        if w is not None:
            ins._wait_ge(w[0], w[1])
        self.n_ins += 1
        kind = "sw" if q is self.pool else "hw"
        t, dr_ = (writes[0], "w") if writes else (reads[0], "r")
        if (dr_, kind) not in t.ds or t.ds[(dr_, kind)][1] >= 3500:
            t.ds[(dr_, kind)] = self.get_dsem(kind)
        d = t.ds[(dr_, kind)]
        d[1] += 1
        ins.then_inc(d[0], 16)
        self._record((d[0], 16 * d[1]), reads, writes)

    def barrier(self):
        tickets = []
        for e in self.engs:
            if e.count > 0:
                tickets.append((e.sem, e.count))
            elif getattr(e, "prev", None):
                tickets.append(e.prev)
        for d in self.all_dsems:
            if d[1]:
                tickets.append((d[0], 16 * d[1]))
        for e in self.engs:
            for (s, v) in tickets:
                e.wait(s, v)

    def close(self):
        while self._stack:
            self._stack.pop().__exit__(None, None, None)
        for cm in reversed(self._sems):
            cm.__exit__(None, None, None)

    def evac(self, out_ap, in_ap, reads, writes):
        self.rr += 1
        if self.rr % 2:
            self.op(self.act, lambda h: h.copy(out=out_ap, in_=in_ap), reads, writes)
        else:
            self.op(self.dve, lambda h: h.tensor_copy(out=out_ap, in_=in_ap), reads, writes)


class Rot:
    def __init__(self, tiles):
        self.tiles = tiles
        self.i = 0

    def next(self):
        t = self.tiles[self.i % len(self.tiles)]
        self.i += 1
        return t


def cdiv(a, b):
    return (a + b - 1) // b


class Cfg:
    def __init__(self, depth=4, ts=4096, n_p=4, stop_after=None, debug=False):
        self.depth = depth
        self.ts = ts
        self.n_p = n_p
        self.tp = n_p * SEQ_P
        self.t = self.ts + self.tp
        self.tks = self.ts + PAST
        self.tk = self.tks + self.tp
        self.rows = ts // GRID_W
        self.stop_after = stop_after
        self.debug = debug
        self.groups = []
        for g in range(ts // 512):
            self.groups.append((g * 512, 512, 0, g * 512))
        for g in range(self.tp // 512):
            self.groups.append((ts + g * 512, 512, 1, ts + g * 512))
        self.sgroups = [g for g in self.groups if g[2] == 0]
        self.pgroups = [g for g in self.groups if g[2] == 1]
        self.kgroups = [(g[0], 512, 0, g[0]) for g in self.sgroups] + [(ts, PAST, 2, ts)] + \
                       [(g[0] + PAST, 512, 1, g[0] + PAST) for g in self.pgroups]

    def kcol(self, t0):
        return t0 if t0 < self.ts else t0 + PAST


def build_program(cfg):
    nc = bass.Bass("TRN2", target_bir_lowering=False)
    P = Prog(nc, debug=cfg.debug)
    L = cfg.depth
    TS, TP, T, TK, TKS = cfg.ts, cfg.tp, cfg.t, cfg.tk, cfg.tks

    def din(name, shape, dtype=F32):
        return nc.dram_tensor(name, list(shape), dtype, kind="ExternalInput").ap()

    def dout(name, shape, dtype=F32):
        return nc.dram_tensor(name, list(shape), dtype, kind="ExternalOutput").ap()

    I = {}
    I["xs"] = din("xs", [TS, D])
    I["xp"] = din("xp", [TP, D])
    I["cond"] = din("cond", [2, D])
    I["c_ckv"] = din("c_ckv", [L, PAST, 256])
    I["c_kr"] = din("c_kr", [L, PAST, 32])
    I["c_nak"] = din("c_nak", [L, PAST, 512])
    I["c_nav"] = din("c_nav", [L, PAST, 512])
    I["c_dk"] = din("c_dk", [L, PAST, 512])
    I["c_dv"] = din("c_dv", [L, PAST, 512])
    I["w_mod"] = din("w_mod", [L, D, 6 * D])
    I["b_mod"] = din("b_mod", [L, 6 * D])
    I["g_norm"] = din("g_norm", [L, 4, D])
    I["w_in"] = din("w_in", [L, D, IN_COLS])
    I["g_q"] = din("g_q", [L, 512])
    I["g_kv"] = din("g_kv", [L, 256])
    I["w_uq"] = din("w_uq", [L, 512, 768])
    I["w_ukv"] = din("w_ukv", [L, 256, 1024])
    I["rpbT"] = din("rpbT", [L, 8, 15, 64, 64])
    I["pool_w"] = din("pool_w", [L, 4, 128, 128])
    I["pool_scale"] = din("pool_scale", [L, 512])
    I["dlam"] = din("dlam", [L, 256])
    I["dng"] = din("dng", [L, 128])
    I["w_br"] = din("w_br", [L, 4, 512, D])
    I["w_o"] = din("w_o", [L, D, D])
    I["w_up"] = din("w_up", [L, D, DFF])
    I["w_down"] = din("w_down", [L, DFF, D])
    I["ident"] = din("ident", [128, 128])
    I["ropeM"] = din("ropeM", [2, 128, TS])
    I["ropeD"] = din("ropeD", [2, 128, TS])
    I["rcS"] = din("rcS", [4, TS])
    I["rcP"] = din("rcP", [4, SEQ_P])
    I["cvt"] = din("cvt", [128, 15, 64])

    O = {}
    O["ys"] = dout("ys", [TS, D])
    O["yp"] = dout("yp", [TP, D])
    O["st_ckv"] = dout("st_ckv", [cfg.n_p, L, SEQ_P, 256])
    O["st_kr"] = dout("st_kr", [cfg.n_p, L, SEQ_P, 32])
    O["st_nak"] = dout("st_nak", [cfg.n_p, L, SEQ_P, 512])
    O["st_nav"] = dout("st_nav", [cfg.n_p, L, SEQ_P, 512])
    O["st_dk"] = dout("st_dk", [cfg.n_p, L, SEQ_P, 512])
    O["st_dv"] = dout("st_dv", [cfg.n_p, L, SEQ_P, 512])

    S = {}
    S["xT"] = P.dram("s_xT", [D, T], F32)
    S["hT"] = P.dram("s_hT", [D, T], BF16)
    S["qkvT"] = P.dram("s_qkvT", [768, T], BF16)
    S["krT"] = P.dram("s_krT", [32, TK], BF16)
    S["naqT"] = P.dram("s_naqT", [512, T], BF16)
    S["nakT"] = P.dram("s_nakT", [512, TK], BF16)
    S["poolT"] = P.dram("s_poolT", [512, T], BF16)
    S["dqT"] = P.dram("s_dqT", [512, T], BF16)
    S["dkT"] = P.dram("s_dkT", [512, TK], BF16)
    S["gT"] = P.dram("s_gT", [8192, T], BF16)
    S["navA"] = P.dram("s_navA", [TK, 1024], BF16)
    S["dvv"] = P.dram("s_dvv", [TK, 512], BF16)
    S["ckvT"] = P.dram("s_ckvT", [256, TK], BF16)
    S["qcnT"] = P.dram("s_qcnT", [512, T], BF16)
    S["qnT"] = P.dram("s_qnT", [512, T], BF16)
    S["qrT"] = P.dram("s_qrT", [256, T], BF16)
    S["knT"] = P.dram("s_knT", [512, TK], BF16)
    S["vmA"] = P.dram("s_vmA", [TK, 1024], BF16)
    S["oT"] = P.dram("s_oT", [D, T], BF16)
    S["mT"] = P.dram("s_mT", [D, T], BF16)
    S["uT"] = P.dram("s_uT", [DFF, T], BF16)
    S["y2T"] = P.dram("s_y2T", [D, T], F32)

    ones_bf = P.sbuf("ones_bf", [128, 128], BF16)
    ident = P.sbuf("ident", [128, 128], F32)
    eps_t = P.sbuf("eps", [128, 1], F32)
    modv = P.sbuf("modv", [128, L * 2 * 6 * 16], F32)
    P.op(P.dve, lambda h: h.memset(ones_bf[:, :], 1.0), writes=[ones_bf])
    P.op(P.dve, lambda h: h.memset(eps_t[:, :], RMS_EPS), writes=[eps_t])
    P.dma(P.sp, ident[:, :], I["ident"], writes=[ident])
    PS = [P.psum("ps%d" % i, [128, 512], F32) for i in range(8)]
    neglam = P.sbuf("neglam", [128, 1], F32)
    dgsc = P.sbuf("dgsc", [128, 1], F32)

    def mv(l, c, v):
        base = ((l * 2 + c) * 6 + v) * 16
        return modv.ap[:, base:base + 16]

    ctx = dict(nc=nc, P=P, cfg=cfg, I=I, O=O, S=S, PS=PS, ones_bf=ones_bf, ident=ident,
               eps_t=eps_t, modv=modv, mv=mv, neglam=neglam, dgsc=dgsc)

    phase_mod(ctx)
    if cfg.stop_after == "mod":
        return finish(ctx)
    phase_xt(ctx)
    if cfg.stop_after == "xt":
        return finish(ctx)
    for l in range(L):
        phase_g1(ctx, l)
        if cfg.stop_after in ("g1", "g1a", "g1g", "g1v"):
            return finish(ctx)
        phase_tr(ctx, l)
        phase_n2(ctx, l)
        phase_g23(ctx, l)
        if cfg.stop_after == "g23":
            return finish(ctx)
        seq_s = [(0, TS, 0, TKS)]
        seq_p = [(TS + j * SEQ_P, SEQ_P, TKS + j * SEQ_P, SEQ_P) for j in range(cfg.n_p)]
        phase_lam(ctx, l)
        attn_dense(ctx, l, "mla", seq_s + seq_p)
        attn_dense(ctx, l, "na", seq_p)
        attn_dense(ctx, l, "diff", seq_s + seq_p)
        phase_na_s(ctx, l)
        phase_pool(ctx, l)
        if cfg.stop_after == "attn":
            return finish(ctx)
        phase_merge(ctx, l)
        phase_wo(ctx, l)
        if cfg.stop_after == "wo":
            return finish(ctx)
        phase_mlp(ctx, l)
    phase_out(ctx)
    return finish(ctx)


def finish(ctx):
    P = ctx["P"]
    if ctx["cfg"].debug:
        dbg = ctx["nc"].dram_tensor("dbg_modv", list(ctx["modv"].ap.shape), F32, kind="ExternalOutput").ap()
        P.dma(P.sp, dbg, ctx["modv"][:, :], reads=[ctx["modv"]])
    P.barrier()
    print("program: n_ins=%d n_wait=%d sems=%d maxdsem=%d engcounts=%s" % (P.n_ins, P.n_wait, len(P._sems),
          max(d[1] * 16 for d in P.all_dsems), [(e.name, e.count) for e in P.engs]))
    P.close()
    return ctx["nc"]


def phase_mod(ctx):
    P, cfg, I, PS, modv, mv = ctx["P"], ctx["cfg"], ctx["I"], ctx["PS"], ctx["modv"], ctx["mv"]
    L = cfg.depth
    m = P.mark()
    condT = P.sbuf("condT", [128, 2, 16], F32)
    scT = P.sbuf("scT", [128, 2, 16], BF16)
    for c in range(2):
        P.dma(P.sp, condT[:, c, :], I["cond"][c].rearrange("(k p) -> p k", p=128), writes=[condT],
              allow_slow_non_contiguous=True)
    P.op(P.act, lambda h: h.activation(out=scT[:, :, :], in_=condT[:, :, :], func=AF.Silu),
         reads=[condT], writes=[scT])
    wm = Rot([P.sbuf("wm%d" % i, [128, 16, 1024], BF16) for i in range(2)])
    bT = P.sbuf("bT", [128, 96], F32)
    gnT = P.sbuf("gnT", [128, 4, 16], F32)
    raw = P.sbuf("rawmod", [128, 2, 96], F32)
    tmp = P.sbuf("tmpmod", [128, 16], F32)
    ps = PS[0]
    for l in range(L):
        P.dma(P.sp, bT[:, :], I["b_mod"][l].rearrange("(c p) -> p c", p=128), writes=[bT],
              allow_slow_non_contiguous=True)
        P.dma(P.sp, gnT[:, :, :], I["g_norm"][l].rearrange("v (c p) -> p v c", p=128), writes=[gnT],
              allow_slow_non_contiguous=True)
        for blk in range(12):
            w = wm.next()
            P.dma(P.pool, w[:, :, :],
                  I["w_mod"][l][:, blk * 1024:(blk + 1) * 1024].rearrange("(c p) n -> p c n", p=128),
                  writes=[w])
            for j in range(8):
                idx = blk * 8 + j
                fns = []
                for kc in range(16):
                    fns.append(lambda h, kc=kc, j=j, w=w, idx=idx: h.matmul(
                        ps[:, idx * 2:idx * 2 + 2], lhsT=w[:, kc, j * 128:(j + 1) * 128], rhs=scT[:, :, kc],
                        start=(kc == 0), stop=(kc == 15)))
                P.mm(fns, reads=[w, scT], writes=[ps], acc=True)
        for c in range(2):
            src = ps.ap[:, 0:192].rearrange("p (i c) -> p c i", c=2)[:, c, :]
            P.op(P.dve, lambda h, c=c, src=src: h.tensor_tensor(out=raw[:, c, :], in0=src, in1=bT[:, :], op=ALU.add),
                 reads=[ps, bT], writes=[raw])
        for c in range(2):
            def rv(v):
                return raw[:, c, v * 16:(v + 1) * 16]
            for (dst, gi, sv) in ((0, 0, 1), (3, 2, 4)):
                P.op(P.dve, lambda h, sv=sv: h.tensor_scalar_add(out=tmp[:, :], in0=rv(sv), scalar1=1.0),
                     reads=[raw], writes=[tmp])
                P.op(P.dve, lambda h, dst=dst, gi=gi: h.tensor_tensor(out=mv(l, c, dst), in0=tmp[:, :], in1=gnT[:, gi, :], op=ALU.mult),
                     reads=[tmp, gnT], writes=[modv])
            for (dst, sv) in ((1, 0), (4, 3)):
                P.op(P.dve, lambda h, dst=dst, sv=sv: h.tensor_copy(out=mv(l, c, dst), in_=rv(sv)),
                     reads=[raw], writes=[modv])
            for (dst, gi, sv) in ((2, 1, 2), (5, 3, 5)):
                P.op(P.dve, lambda h, dst=dst, gi=gi, sv=sv: h.tensor_tensor(out=mv(l, c, dst), in0=rv(sv), in1=gnT[:, gi, :], op=ALU.mult),
                     reads=[raw, gnT], writes=[modv])
    P.barrier()
    P.release(m)


def rstd_from_chunks(ctx, chunks, nfeat, n, ss_ps, rstd, sq_rot, src_tiles):
    P, ones_bf, eps_t = ctx["P"], ctx["ones_bf"], ctx["eps_t"]
    nch = len(chunks)
    for i, ch in enumerate(chunks):
        np_ = ch.shape[0]
        sq = sq_rot.next()
        P.op(P.act, lambda h, ch=ch, sq=sq, np_=np_: h.activation(out=sq[0:np_, 0:n], in_=ch, func=AF.Square),
             reads=src_tiles, writes=[sq])
        P.mm([lambda h, sq=sq, i=i, np_=np_: h.matmul(ss_ps[:, 0:n], lhsT=ones_bf[0:np_, :], rhs=sq[0:np_, 0:n],
                                                     start=(i == 0), stop=(i == nch - 1))],
             reads=[sq, ones_bf], writes=[ss_ps], acc=(i > 0))
    P.op(P.act, lambda h: h.activation(out=rstd[:, 0:n], in_=ss_ps[:, 0:n], func=AF.Sqrt, bias=eps_t[:, 0:1],
                                       scale=1.0 / nfeat),
         reads=[ss_ps, eps_t], writes=[rstd])
    P.op(P.dve, lambda h: h.reciprocal(out=rstd[:, 0:n], in_=rstd[:, 0:n]), reads=[rstd], writes=[rstd])


def mod_norm_group(ctx, xg, n, A, B, hg, ss_ps, rstd, sq_rot, tmp_rot):
    P = ctx["P"]
    rstd_from_chunks(ctx, [xg[:, fc, 0:n] for fc in range(16)], D, n, ss_ps, rstd, sq_rot, [xg])
    for fc in range(16):
        tmp = tmp_rot.next()
        P.op(P.dve, lambda h, fc=fc, tmp=tmp: h.tensor_tensor(out=tmp[:, 0:n], in0=xg[:, fc, 0:n], in1=rstd[:, 0:n], op=ALU.mult),
             reads=[xg, rstd], writes=[tmp])
        P.op(P.act, lambda h, fc=fc, tmp=tmp: h.activation(out=hg[:, fc, 0:n], in_=tmp[:, 0:n], func=AF.Identity,
                                                           bias=B[:, fc:fc + 1], scale=A[:, fc:fc + 1]),
             reads=[tmp, ctx["modv"]], writes=[hg])


def phase_xt(ctx):
    P, cfg, I, S, PS, ident, mv = ctx["P"], ctx["cfg"], ctx["I"], ctx["S"], ctx["PS"], ctx["ident"], ctx["mv"]
    m = P.mark()
    onesb = P.sbuf("onesb", [128, 512], BF16)
    P.op(P.dve, lambda h: h.memset(onesb[:, :], 1.0), writes=[onesb])
    for nm in ("navA", "vmA"):
        for r0 in range(0, cfg.tk, 128):
            P.dma(P.sp, S[nm][r0:r0 + 128, :].rearrange("k (h j) -> k h j", j=128)[:, :, 64:128],
                  onesb[:, :].rearrange("p (h j) -> p h j", j=64), reads=[onesb])
    xin = Rot([P.sbuf("xin%d" % i, [128, D], F32) for i in range(5)])
    xg_rot = Rot([P.sbuf("xg%d" % i, [128, 16, 512], F32) for i in range(2)])
    hg_rot = Rot([P.sbuf("hg%d" % i, [128, 16, 512], BF16) for i in range(2)])
    sq_rot = Rot([P.sbuf("sq%d" % i, [128, 512], BF16) for i in range(3)])
    tmp_rot = Rot([P.sbuf("tmp%d" % i, [128, 512], F32) for i in range(3)])
    rstd = P.sbuf("rstd", [128, 512], F32)
    ps_rot = Rot(PS[0:6])
    ss_ps = PS[7]
    xT_v = S["xT"].rearrange("(c p) t -> p c t", p=128)
    hT_v = S["hT"].rearrange("(c p) t -> p c t", p=128)
    for (t0, n, c, _) in cfg.groups:
        src = I["xs"] if c == 0 else I["xp"]
        r0 = t0 if c == 0 else t0 - cfg.ts
        tiles = []
        for i in range(n // 128):
            xt = xin.next()
            P.dma(P.sp, xt[:, :], src[r0 + i * 128:r0 + (i + 1) * 128, :], writes=[xt])
            tiles.append(xt)
        xg = xg_rot.next()
        for fc in range(16):
            ps = ps_rot.next()
            for i, xt in enumerate(tiles):
                P.mm([lambda h, xt=xt, i=i, fc=fc, ps=ps: h.transpose(out=ps[:, i * 128:(i + 1) * 128],
                                                                     in_=xt[:, fc * 128:(fc + 1) * 128], identity=ident[:, :])],
                     reads=[xt, ident], writes=[ps], acc=(i > 0))
            P.evac(xg[:, fc, 0:n], ps[:, 0:n], [ps], [xg])
        P.dma(P.pool, xT_v[:, :, t0:t0 + n], xg[:, :, 0:n], reads=[xg])
        hg = hg_rot.next()
        mod_norm_group(ctx, xg, n, mv(0, c, 0), mv(0, c, 1), hg, ss_ps, rstd, sq_rot, tmp_rot)
        P.dma(P.pool, hT_v[:, :, t0:t0 + n], hg[:, :, 0:n], reads=[hg])
    P.barrier()
    P.release(m)


def wsegs_to_blocks(segs, wmax):
    blocks, cur, used = [], [], 0
    for (ap, m) in segs:
        w = ap.shape[-1] if len(ap.shape) == 2 else ap.shape[1] * ap.shape[2]
        c0 = 0
        while c0 < w:
            room = wmax - used
            if room < m:
                blocks.append(cur)
                cur, used = [], 0
                room = wmax
            take = min(w - c0, (room // m) * m)
            cur.append((ap, m, used, c0, take))
            used += take
            c0 += take
    if cur:
        blocks.append(cur)
    return blocks


def load_wblock(ctx, wt, blk, K):
    P = ctx["P"]
    KC = K // 128
    for (ap, m, off, c0, take) in blk:
        if len(ap.shape) == 2:
            src = ap[:, c0:c0 + take].rearrange("(c p) n -> p c n", p=128)
            if KC <= 16:
                P.dma(P.pool, wt[:, 0:KC, off:off + take], src, writes=[wt])
            else:
                for k0 in range(0, KC, 16):
                    P.dma(P.pool, wt[:, k0:k0 + 16, off:off + take], src[:, k0:k0 + 16, :], writes=[wt])
        else:
            assert c0 == 0 and take == ap.shape[1] * ap.shape[2]
            for kc in range(KC):
                P.dma(P.pool, wt[:, kc, off:off + take].rearrange("p (a b) -> p a b", b=ap.shape[2]),
                      ap[kc * 128:(kc + 1) * 128, :, :], writes=[wt])


def gemm_F(ctx, segs, K, act, groups, epi, wmax, kpiece=None, ps_list=None, abufs=2):
    P = ctx["P"]
    KC = K // 128
    kpiece = kpiece or KC
    npieces = KC // kpiece
    m0 = P.mark()
    wt = P.sbuf("wblk", [128, KC, wmax], BF16)
    arot = Rot([P.sbuf("act%d" % i, [128, kpiece, 512], BF16) for i in range(abufs)])
    ps_list = ps_list or ctx["PS"][0:6]
    psrot = Rot(ps_list)
    act_v = act.rearrange("(c p) t -> p c t", p=128)
    blocks = wsegs_to_blocks(segs, wmax)
    ti_base = 0
    for blk in blocks:
        load_wblock(ctx, wt, blk, K)
        tiles = []
        for (ap, m, off, c0, take) in blk:
            for j in range(take // m):
                tiles.append((off + j * m, m))
        units = [(g, pc) for g in groups for pc in range(npieces)]
        loaded = {}

        def load(u):
            g, pc = u
            a = arot.next()
            P.dma(P.sp, a[:, :, 0:g[1]], act_v[:, pc * kpiece:(pc + 1) * kpiece, g[3]:g[3] + g[1]], writes=[a])
            loaded[u] = a
        load(units[0])
        pss = None
        for ui, u in enumerate(units):
            if ui + 1 < len(units):
                load(units[ui + 1])
            g, pc = u
            n = g[1]
            a = loaded.pop(u)
            if npieces == 1:
                for ti, (c, m) in enumerate(tiles):
                    ps = psrot.next()
                    fns = [lambda h, kc=kc, c=c, m=m, ps=ps, a=a: h.matmul(ps[0:m, 0:n], lhsT=wt[:, kc, c:c + m], rhs=a[:, kc, 0:n],
                                                                       start=(kc == 0), stop=(kc == KC - 1)) for kc in range(KC)]
                    P.mm(fns, reads=[wt, a], writes=[ps])
                    epi(ti_base + ti, ps, m, g)
            else:
                if pc == 0:
                    pss = [psrot.next() for _ in tiles]
                for ti, (c, m) in enumerate(tiles):
                    ps = pss[ti]
                    fns = [lambda h, kc=kc, c=c, m=m, ps=ps, a=a, pc=pc: h.matmul(
                        ps[0:m, 0:n], lhsT=wt[:, pc * kpiece + kc, c:c + m], rhs=a[:, kc, 0:n],
                        start=(pc == 0 and kc == 0), stop=(pc == npieces - 1 and kc == kpiece - 1)) for kc in range(kpiece)]
                    P.mm(fns, reads=[wt, a], writes=[ps], acc=(pc > 0))
                if pc == npieces - 1:
                    for ti, (c, m) in enumerate(tiles):
                        epi(ti_base + ti, pss[ti], m, g)
        ti_base += len(tiles)
    return m0


def gemm_T(ctx, segs, K, act, groups, epi, wmax):
    P = ctx["P"]
    KC = K // 128
    m0 = P.mark()
    wt = P.sbuf("wblkT", [128, KC, wmax], BF16)
    arot = Rot([P.sbuf("actT%d" % i, [128, KC, 512], BF16) for i in range(2)])
    psrot = Rot(ctx["PS"][0:6])
    act_v = act.rearrange("(c p) t -> p c t", p=128)
    blocks = wsegs_to_blocks(segs, wmax)
    ci_base = 0
    for blk in blocks:
        load_wblock(ctx, wt, blk, K)
        chunks = []
        for (ap, m, off, c0, take) in blk:
            for j in range(take // m):
                chunks.append((off + j * m, m))
        loaded = {}

        def load(gi):
            g = groups[gi]
            a = arot.next()
            P.dma(P.sp, a[:, :, 0:g[1]], act_v[:, :, g[3]:g[3] + g[1]], writes=[a])
            loaded[gi] = a
        load(0)
        for gi, g in enumerate(groups):
            if gi + 1 < len(groups):
                load(gi + 1)
            a = loaded.pop(gi)
            for tt in range(g[1] // 128):
                for ci, (c, cw) in enumerate(chunks):
                    ps = psrot.next()
                    fns = [lambda h, kc=kc, c=c, cw=cw, ps=ps, a=a, tt=tt: h.matmul(
                        ps[:, 0:cw], lhsT=a[:, kc, tt * 128:(tt + 1) * 128], rhs=wt[:, kc, c:c + cw],
                        start=(kc == 0), stop=(kc == KC - 1)) for kc in range(KC)]
                    P.mm(fns, reads=[wt, a], writes=[ps])
                    epi(ci_base + ci, ps, cw, g, tt)
        ci_base += len(chunks)
    return m0


class RopeTabs:
    def __init__(self, ctx, name, table_ap, nrows):
        P = ctx["P"]
        self.ctx, self.ap, self.nrows = ctx, table_ap, nrows
        self.rot = Rot([(P.sbuf(name + "c%d" % i, [128, 512], F32), P.sbuf(name + "s%d" % i, [128, 512], F32)) for i in range(2)])
        self.key = None
        self.cur = None

    def get(self, key, t0, n):
        if key != self.key:
            P = self.ctx["P"]
            c, s_ = self.rot.next()
            P.dma(P.sp, c[0:self.nrows, 0:n], self.ap[0][0:self.nrows, t0:t0 + n], writes=[c])
            P.dma(P.sp, s_[0:self.nrows, 0:n], self.ap[1][0:self.nrows, t0:t0 + n], writes=[s_])
            self.key, self.cur = key, (c, s_)
        return self.cur


def rope_pair(ctx, ps1, ps2, m, n, cs, tmp_rot, st_rot, store1, store2):
    P = ctx["P"]
    if cs is None:
        for ps, store in ((ps1, store1), (ps2, store2)):
            st = st_rot.next()
            P.evac(st[0:m, 0:n], ps[0:m, 0:n], [ps], [st])
            store(st)
        return
    c, s_ = cs
    for (a, b, op, store) in ((ps1, ps2, ALU.subtract, store1), (ps2, ps1, ALU.add, store2)):
        t1 = tmp_rot.next()
        t2 = tmp_rot.next()
        P.op(P.dve, lambda h, a=a, t1=t1: h.tensor_tensor(out=t1[0:m, 0:n], in0=a[0:m, 0:n], in1=c[0:m, 0:n], op=ALU.mult),
             reads=[a, c], writes=[t1])
        P.op(P.dve, lambda h, b=b, t2=t2: h.tensor_tensor(out=t2[0:m, 0:n], in0=b[0:m, 0:n], in1=s_[0:m, 0:n], op=ALU.mult),
             reads=[b, s_], writes=[t2])
        st = st_rot.next()
        P.op(P.pool, lambda h, t1=t1, t2=t2, st=st, op=op: h.tensor_tensor(out=st[0:m, 0:n], in0=t1[0:m, 0:n], in1=t2[0:m, 0:n], op=op),
             reads=[t1, t2], writes=[st])
        store(st)


def phase_g1(ctx, l):
    P, cfg, I, O, S = ctx["P"], ctx["cfg"], ctx["I"], ctx["O"], ctx["S"]
    W = I["w_in"][l]
    TS = cfg.ts

    def perm(c0, x0):
        return W[:, c0:c0 + 512].rearrange("k (h x) -> k h x", x=128)[:, :, x0:x0 + 32]

    m = P.mark()
    st_rot = Rot([P.sbuf("st%d" % i, [128, 512], BF16) for i in range(6)])
    tmp_rot = Rot([P.sbuf("rt%d" % i, [128, 512], F32) for i in range(4)])
    ropeD = RopeTabs(ctx, "rD", I["ropeD"], 128)
    ropeK = RopeTabs(ctx, "rK", I["ropeM"], 16)
    segs = [(W[:, 0:768], 128), (W[:, 768:800], 16), (W[:, 800:1312], 128), (W[:, 1312:1824], 128),
            (W[:, 2336:2848], 128)]
    for c0 in (2848, 3360):
        for x0 in (0, 32, 64, 96):
            segs.append((perm(c0, x0), 128))
    pend = {}

    def store_plain(dst, t0, n, mrows):
        def f(st):
            P.dma(P.sp, dst[:, t0:t0 + n], st[0:mrows, 0:n], reads=[st])
        return f

    def epi(ti, ps, mm_, g):
        t0, n, c, _ = g
        kc0 = cfg.kcol(t0)
        if ti < 6 or (8 <= ti < 20):
            st = st_rot.next()
            P.evac(st[0:mm_, 0:n], ps[0:mm_, 0:n], [ps], [st])
            if ti < 6:
                dst, col = S["qkvT"][ti * 128:(ti + 1) * 128], t0
            elif ti < 12:
                dst, col = S["naqT"][(ti - 8) * 128:(ti - 7) * 128], t0
            elif ti < 16:
                dst, col = S["nakT"][(ti - 12) * 128:(ti - 11) * 128], kc0
            else:
                dst, col = S["poolT"][(ti - 16) * 128:(ti - 15) * 128], t0
            P.dma(P.sp, dst[:, col:col + n], st[0:mm_, 0:n], reads=[st])
            return
        if ti in (6, 20, 22, 24, 26):
            pend[0] = ps
            return
        ps1 = pend.pop(0)
        if ti == 7:
            cs = ropeK.get(("k", t0), t0, n) if c == 0 else None
            rope_pair(ctx, ps1, ps, 16, n, cs, tmp_rot, st_rot,
                      store_plain(S["krT"][0:16], kc0, n, 16), store_plain(S["krT"][16:32], kc0, n, 16))
        else:
            cs = ropeD.get(("d", t0), t0, n) if c == 0 else None
            half = 0 if ti in (21, 25) else 1
            if ti < 24:
                dst, col = S["dqT"], t0
            else:
                dst, col = S["dkT"], kc0
            r1 = (half * 2 + 0) * 128
            r2 = (half * 2 + 1) * 128
            rope_pair(ctx, ps1, ps, 128, n, cs, tmp_rot, st_rot,
                      store_plain(dst[r1:r1 + 128], col, n, 128), store_plain(dst[r2:r2 + 128], col, n, 128))

    gemm_F(ctx, segs, D, S["hT"], cfg.groups, epi, wmax=2048)
    P.barrier()
    P.release(m)
    if cfg.stop_after == "g1a":
        return

    m = P.mark()
    st_rot = Rot([P.sbuf("stg%d" % i, [128, 512], BF16) for i in range(4)])

    def epi_g(ti, ps, mm_, g):
        t0, n, c, _ = g
        st = st_rot.next()
        P.op(P.act, lambda h: h.activation(out=st[:, 0:n], in_=ps[:, 0:n], func=AF.Sigmoid), reads=[ps], writes=[st])
        P.dma(P.sp, S["gT"][ti * 128:(ti + 1) * 128, t0:t0 + n], st[:, 0:n], reads=[st])

    gemm_F(ctx, [(W[:, 4384:IN_COLS], 128)], D, S["hT"], cfg.groups, epi_g, wmax=2048)
    P.barrier()
    P.release(m)
    if cfg.stop_after == "g1g":
        return

    m = P.mark()
    stv_rot = Rot([P.sbuf("stv%d" % i, [128, 512], BF16) for i in range(4)])
    f32_rot = Rot([P.sbuf("stf%d" % i, [128, 512], F32) for i in range(3)])

    def state_dst(name, g, tt, w):
        t0 = g[0] + tt * 128 - TS
        seq, r = t0 // SEQ_P, t0 % SEQ_P
        return O[name][seq, l, r:r + 128, 0:w]

    def epi_v(ci, ps, cw, g, tt):
        t0, n, c, _ = g
        krow = cfg.kcol(t0) + tt * 128
        if ci == 0:
            st = stv_rot.next()
            P.evac(st[:, :], ps[:, 0:512], [ps], [st])
            P.dma(P.sp, S["navA"][krow:krow + 128, :].rearrange("k (h j) -> k h j", j=128)[:, :, 0:64],
                  st[:, :].rearrange("p (h j) -> p h j", j=64), reads=[st])
        else:
            st = stv_rot.next()
            P.evac(st[:, :], ps[:, 0:512], [ps], [st])
            P.dma(P.sp, S["dvv"][krow:krow + 128, :], st[:, :], reads=[st])
        import os as _os
        if c == 1 and "states" not in _os.environ.get("K_SKIP", ""):
            sf = f32_rot.next()
            P.evac(sf[:, :], ps[:, 0:512], [ps], [sf])
            P.dma(P.sp, state_dst("st_nav" if ci == 0 else "st_dv", g, tt, 512), sf[:, :], reads=[sf])

    gemm_T(ctx, [(W[:, 1824:2336], 512), (W[:, 3872:4384], 512)], D, S["hT"], cfg.groups, epi_v, wmax=1024)
    P.barrier()
    P.release(m)
    if cfg.stop_after == "g1v":
        return

    if cfg.pgroups:
        m = P.mark()
        f32_rot = Rot([P.sbuf("stf%d" % i, [128, 512], F32) for i in range(4)])
        junk = P.sbuf("junk", [128, 256], F32)
        ss = P.sbuf("ss", [128, 1], F32)
        gkv = P.sbuf("gkvb", [128, 256], F32)
        P.dma(P.sp, gkv[:, :], I["g_kv"][l:l + 1, :].to_broadcast([128, 256]), writes=[gkv])

        def epi_s(ci, ps, cw, g, tt):
            sf = f32_rot.next()
            if ci == 0:
                P.op(P.dve, lambda h: h.memset(ss[:, :], 0.0), writes=[ss])
                P.op(P.act, lambda h: h.activation(out=junk[:, :], in_=ps[:, 0:256], func=AF.Square, accum_out=ss[:, 0:1]),
                     reads=[ps], writes=[junk, ss])
                P.op(P.act, lambda h: h.activation(out=ss[:, :], in_=ss[:, :], func=AF.Sqrt, bias=ctx["eps_t"][:, 0:1], scale=1.0 / 256),
                     reads=[ss, ctx["eps_t"]], writes=[ss])
                P.op(P.dve, lambda h: h.reciprocal(out=ss[:, :], in_=ss[:, :]), reads=[ss], writes=[ss])
                P.op(P.dve, lambda h: h.scalar_tensor_tensor(out=sf[:, 0:256], in0=ps[:, 0:256], scalar=ss[:, 0:1], in1=gkv[:, :],
                                                             op0=ALU.mult, op1=ALU.mult), reads=[ps, ss, gkv], writes=[sf])
                P.dma(P.sp, state_dst("st_ckv", g, tt, 256), sf[:, 0:256], reads=[sf])
            else:
                P.evac(sf[:, 0:cw], ps[:, 0:cw], [ps], [sf])
                name = {1: "st_kr", 2: "st_nak", 3: "st_dk"}[ci]
                P.dma(P.sp, state_dst(name, g, tt, cw), sf[:, 0:cw], reads=[sf])

        gemm_T(ctx, [(W[:, 512:768], 256), (W[:, 768:800], 32), (W[:, 1312:1824], 512), (W[:, 3360:3872], 512)],
               D, S["hT"], cfg.pgroups, epi_s, wmax=1312)
        P.barrier()
        P.release(m)


def phase_tr(ctx, l):
    P, cfg, I, S, PS, ident = ctx["P"], ctx["cfg"], ctx["I"], ctx["S"], ctx["PS"], ctx["ident"]
    TS = cfg.ts
    m = P.mark()
    xin_rot = Rot([P.sbuf("trin%d" % i, [128, 4, 512], F32) for i in range(2)])
    st_rot = Rot([P.sbuf("trst%d" % i, [128, 512], BF16) for i in range(4)])
    psrot = Rot(PS[0:6])

    def tr(src, nfeat, dst_fn):
        x = xin_rot.next()
        P.dma(P.sp, x[:, :, 0:nfeat], src.rearrange("(t p) f -> p t f", p=128), writes=[x])
        for fc in range(cdiv(nfeat, 128)):
            fw = min(128, nfeat - fc * 128)
            ps = psrot.next()
            for tt in range(4):
                P.mm([lambda h, tt=tt, fc=fc, fw=fw, ps=ps: h.transpose(out=ps[0:fw, tt * 128:(tt + 1) * 128],
                                                                        in_=x[:, tt, fc * 128:fc * 128 + fw], identity=ident[:, :])],
                     reads=[x, ident], writes=[ps], acc=(tt > 0))
            st = st_rot.next()
            P.evac(st[0:fw, :], ps[0:fw, :], [ps], [st])
            dst_fn(fc, st, fw)

    ctxc = slice(TS, TS + PAST)
    tr(I["c_ckv"][l], 256, lambda fc, st, fw: P.dma(P.sp, S["ckvT"][fc * 128:(fc + 1) * 128, ctxc], st[:, :], reads=[st]))
    tr(I["c_kr"][l], 32, lambda fc, st, fw: P.dma(P.sp, S["krT"][0:32, ctxc], st[0:32, :], reads=[st]))
    tr(I["c_nak"][l], 512, lambda fc, st, fw: P.dma(P.sp, S["nakT"][fc * 128:(fc + 1) * 128, ctxc], st[:, :], reads=[st]))

    def dk_store(fc, st, fw):
        for hx in range(4):
            r = (hx * 4 + fc) * 32
            P.dma(P.sp, S["dkT"][r:r + 32, ctxc], st[hx * 32:(hx + 1) * 32, :], reads=[st])
    tr(I["c_dk"][l], 512, dk_store)

    vtmp = P.sbuf("trv", [128, 4, 512], BF16)
    P.dma(P.pool, vtmp[:, :, :], I["c_nav"][l].rearrange("(t p) f -> p t f", p=128), writes=[vtmp])
    for tt in range(4):
        P.dma(P.sp, S["navA"][TS + tt * 128:TS + (tt + 1) * 128, :].rearrange("k (h j) -> k h j", j=128)[:, :, 0:64],
              vtmp[:, tt, :].rearrange("p (h j) -> p h j", j=64), reads=[vtmp])
    vtmp2 = P.sbuf("trv2", [128, 4, 512], BF16)
    P.dma(P.pool, vtmp2[:, :, :], I["c_dv"][l].rearrange("(t p) f -> p t f", p=128), writes=[vtmp2])
    P.dma(P.sp, S["dvv"][TS:TS + PAST, :].rearrange("(t p) f -> p t f", p=128), vtmp2[:, :, :], reads=[vtmp2])
    P.barrier()
    P.release(m)


def phase_n2(ctx, l):
    P, cfg, I, S, PS = ctx["P"], ctx["cfg"], ctx["I"], ctx["S"], ctx["PS"]
    m = P.mark()
    gq = P.sbuf("gq", [128, 4], F32)
    gkv = P.sbuf("gkv", [128, 2], F32)
    P.dma(P.sp, gq[:, :], I["g_q"][l].rearrange("(c p) -> p c", p=128), writes=[gq], allow_slow_non_contiguous=True)
    P.dma(P.sp, gkv[:, :], I["g_kv"][l].rearrange("(c p) -> p c", p=128), writes=[gkv], allow_slow_non_contiguous=True)
    xrot = Rot([P.sbuf("n2x%d" % i, [128, 6, 512], BF16) for i in range(2)])
    orot = Rot([P.sbuf("n2o%d" % i, [128, 6, 512], BF16) for i in range(2)])
    sq_rot = Rot([P.sbuf("n2sq%d" % i, [128, 512], BF16) for i in range(3)])
    rs = [P.sbuf("n2r%d" % i, [128, 512], F32) for i in range(2)]
    src_v = S["qkvT"].rearrange("(c p) t -> p c t", p=128)
    qdst = S["qcnT"].rearrange("(c p) t -> p c t", p=128)
    kdst = S["ckvT"].rearrange("(c p) t -> p c t", p=128)
    for (t0, n, c, _) in cfg.groups:
        x = xrot.next()
        P.dma(P.sp, x[:, :, 0:n], src_v[:, :, t0:t0 + n], writes=[x])
        o = orot.next()
        rstd_from_chunks(ctx, [x[:, fc, 0:n] for fc in range(4)], 512, n, PS[6], rs[0], sq_rot, [x])
        rstd_from_chunks(ctx, [x[:, fc, 0:n] for fc in range(4, 6)], 256, n, PS[7], rs[1], sq_rot, [x])
        for fc in range(6):
            g_ap = gq[:, fc:fc + 1] if fc < 4 else gkv[:, fc - 4:fc - 3]
            r = rs[0] if fc < 4 else rs[1]
            P.op(P.dve, lambda h, fc=fc, g_ap=g_ap, r=r: h.scalar_tensor_tensor(
                out=o[:, fc, 0:n], in0=x[:, fc, 0:n], scalar=g_ap, in1=r[:, 0:n], op0=ALU.mult, op1=ALU.mult),
                reads=[x, r, gq, gkv], writes=[o])
        P.dma(P.sp, qdst[:, :, t0:t0 + n], o[:, 0:4, 0:n], reads=[o])
        kc0 = cfg.kcol(t0)
        P.dma(P.sp, kdst[:, :, kc0:kc0 + n], o[:, 4:6, 0:n], reads=[o])
    P.barrier()
    P.release(m)


def phase_g23(ctx, l):
    P, cfg, I, S = ctx["P"], ctx["cfg"], ctx["I"], ctx["S"]
    m = P.mark()
    st_rot = Rot([P.sbuf("st%d" % i, [128, 512], BF16) for i in range(6)])
    tmp_rot = Rot([P.sbuf("rt%d" % i, [128, 512], F32) for i in range(4)])
    ropeM = RopeTabs(ctx, "rM", I["ropeM"], 128)
    Wq = I["w_uq"][l]
    Wq3 = Wq.rearrange("k (h x) -> k h x", x=96)
    segs = [(Wq[:, h * 96:h * 96 + 64], 64) for h in range(8)] + [(Wq3[:, :, 64:80], 128), (Wq3[:, :, 80:96], 128)]
    pend = {}

    def epi(ti, ps, mm_, g):
        t0, n, c, _ = g
        if ti < 8:
            st = st_rot.next()
            P.evac(st[0:64, 0:n], ps[0:64, 0:n], [ps], [st])
            P.dma(P.sp, S["qnT"][ti * 64:(ti + 1) * 64, t0:t0 + n], st[0:64, 0:n], reads=[st])
        elif ti == 8:
            pend[0] = ps
        else:
            cs = ropeM.get(("m", t0), t0, n) if c == 0 else None
            rope_pair(ctx, pend.pop(0), ps, 128, n, cs, tmp_rot, st_rot,
                      lambda st: P.dma(P.sp, S["qrT"][0:128, t0:t0 + n], st[:, 0:n], reads=[st]),
                      lambda st: P.dma(P.sp, S["qrT"][128:256, t0:t0 + n], st[:, 0:n], reads=[st]))

    gemm_F(ctx, segs, 512, S["qcnT"], cfg.groups, epi, wmax=768)
    P.barrier()
    P.release(m)

    m = P.mark()
    st_rot = Rot([P.sbuf("st%d" % i, [128, 512], BF16) for i in range(4)])
    Wk = I["w_ukv"][l]

    def epi_k(ti, ps, mm_, g):
        k0, n = g[0], g[1]
        st = st_rot.next()
        P.evac(st[0:64, 0:n], ps[0:64, 0:n], [ps], [st])
        P.dma(P.sp, S["knT"][ti * 64:(ti + 1) * 64, k0:k0 + n], st[0:64, 0:n], reads=[st])

    gemm_F(ctx, [(Wk[:, h * 128:h * 128 + 64], 64) for h in range(8)], 256, S["ckvT"], cfg.kgroups, epi_k, wmax=512)
    P.barrier()
    P.release(m)

    m = P.mark()
    aug_rot = Rot([P.sbuf("aug%d" % i, [128, 512], BF16) for i in range(4)])

    def epi_v(ci, ps, cw, g, tt):
        krow = g[0] + tt * 128
        st = aug_rot.next()
        P.evac(st[:, :], ps[:, 0:512], [ps], [st])
        P.dma(P.sp, S["vmA"][krow:krow + 128, :].rearrange("k (h j) -> k h j", j=128)[:, :, 0:64],
              st[:, :].rearrange("p (h j) -> p h j", j=64), reads=[st])

    gemm_T(ctx, [(Wk.rearrange("k (h x) -> k h x", x=128)[:, :, 64:128], 512)], 256, S["ckvT"], cfg.kgroups, epi_v, wmax=512)
    P.barrier()
    P.release(m)


def lam_init_of(l):
    return 0.8 - 0.6 * math.exp(-0.3 * l)


def attn_dense(ctx, l, kind, seqs):
    P, cfg, I, S, PS, ones_bf, eps_t = ctx["P"], ctx["cfg"], ctx["I"], ctx["S"], ctx["PS"], ctx["ones_bf"], ctx["eps_t"]
    m = P.mark()
    nkmax = max(sq[3] for sq in seqs)
    nktmax = nkmax // 128
    diff = kind == "diff"
    if kind == "mla":
        heads, d, scale, obase, ow = 8, 96, 96 ** -0.5, 0, 64
    elif kind == "na":
        heads, d, scale, obase, ow = 8, 64, 64 ** -0.5, 512, 64
    else:
        heads, d, scale, obase, ow = 4, 64, 64 ** -0.5, 1536, 128
    nmaps = 2 if diff else 1

    def kparts(h, mp):
        if kind == "mla":
            return [(S["knT"][h * 64:(h + 1) * 64], 0, 64), (S["krT"][0:32], 64, 32)]
        if kind == "na":
            return [(S["nakT"][h * 64:(h + 1) * 64], 0, 64)]
        return [(S["dkT"][(mp * 2 + x) * 128 + h * 32:(mp * 2 + x) * 128 + h * 32 + 32], x * 32, 32) for x in range(2)]

    def qparts(h, mp):
        if kind == "mla":
            return [(S["qnT"][h * 64:(h + 1) * 64], 0, 64), (S["qrT"][h * 16:(h + 1) * 16], 64, 16),
                    (S["qrT"][128 + h * 16:128 + (h + 1) * 16], 80, 16)]
        if kind == "na":
            return [(S["naqT"][h * 64:(h + 1) * 64], 0, 64)]
        return [(S["dqT"][(mp * 2 + x) * 128 + h * 32:(mp * 2 + x) * 128 + h * 32 + 32], x * 32, 32) for x in range(2)]

    def vsrc(h):
        if kind == "mla":
            return S["vmA"][:, h * 128:(h + 1) * 128]
        if kind == "na":
            return S["navA"][:, h * 128:(h + 1) * 128]
        return S["dvv"][:, h * 128:(h + 1) * 128]

    krot = Rot([[P.sbuf("kT%d_%d" % (i, mp), [128, nkmax], BF16) for mp in range(nmaps)] for i in range(2)])
    vrot = Rot([P.sbuf("vt%d" % i, [128, nktmax, 128], BF16) for i in range(2)])
    qrot = Rot([[P.sbuf("qT%d_%d" % (i, mp), [128, 512], BF16) for mp in range(nmaps)] for i in range(2)])
    prot = Rot([P.sbuf("pT%d" % i, [128, 512], BF16) for i in range(4)])
    sqrot = Rot([P.sbuf("asq%d" % i, [128, 512], BF16) for i in range(2)])
    strot = Rot([P.sbuf("ast%d" % i, [128, 512], BF16) for i in range(3)])
    f32rot = Rot([P.sbuf("af%d" % i, [128, 512], F32) for i in range(5 if diff else 2)])
    kmp = [P.sbuf("kmp%d" % mp, [128, 16], F32) for mp in range(nmaps)]
    km = [P.sbuf("km%d" % mp, [128, 1], F32) for mp in range(nmaps)]
    negc_rot = Rot([[P.sbuf("negc%d_%d" % (i, mp), [128, 1], F32) for mp in range(nmaps)] for i in range(2)])
    srot = Rot(PS[0:4])
    if diff:
        accs = [PS[4:8]]
        neglam, gsc = ctx["neglam"], ctx["dgsc"]
    else:
        accs = [[PS[4]], [PS[5]]]
    accrot = Rot(accs)
    stat_ps = PS[6]

    def sqmax(src_tile, dd, n, dst_ap):
        sq = sqrot.next()
        P.op(P.act, lambda h: h.activation(out=sq[0:dd, 0:n], in_=src_tile[0:dd, 0:n], func=AF.Square), reads=[src_tile], writes=[sq])
        ps = srot.next() if diff else stat_ps
        P.mm([lambda h: h.matmul(ps[:, 0:n], lhsT=ones_bf[0:dd, :], rhs=sq[0:dd, 0:n], start=True, stop=True)],
             reads=[sq, ones_bf], writes=[ps])
        P.op(P.dve, lambda h: h.reduce_max(out=dst_ap, in_=ps[:, 0:n], axis=AX.X), reads=[ps], writes=[dst_tile_of[id(dst_ap)]])

    dst_tile_of = {}

    for (q0, nq, k0, nk) in seqs:
        nkt = nk // 128
        for h in range(heads):
            kt = krot.next()
            vt = vrot.next()
            for mp in range(nmaps):
                for (ap, r0, nr) in kparts(h, mp):
                    P.dma(P.sp, kt[mp][r0:r0 + nr, 0:nk], ap[:, k0:k0 + nk], writes=[kt[mp]])
            for c8 in range(0, nkt, 8):
                e8 = min(nkt, c8 + 8)
                P.dma(P.sp, vt[:, c8:e8, :], vsrc(h)[k0 + c8 * 128:k0 + e8 * 128, :].rearrange("(t p) j -> p t j", p=128), writes=[vt])
            for mp in range(nmaps):
                nch = cdiv(nk, 512)
                for ci in range(nch):
                    c0 = ci * 512
                    n_ = min(512, nk - c0)
                    dst = kmp[mp][:, ci:ci + 1]
                    dst_tile_of[id(dst)] = kmp[mp]
                    sq = sqrot.next()
                    P.op(P.act, lambda h_, sq=sq, mp=mp, c0=c0, n_=n_: h_.activation(out=sq[0:d, 0:n_], in_=kt[mp][0:d, c0:c0 + n_], func=AF.Square),
                         reads=[kt[mp]], writes=[sq])
                    ps = srot.next() if diff else stat_ps
                    P.mm([lambda h_, ps=ps, sq=sq, n_=n_: h_.matmul(ps[:, 0:n_], lhsT=ones_bf[0:d, :], rhs=sq[0:d, 0:n_], start=True, stop=True)],
                         reads=[sq, ones_bf], writes=[ps])
                    P.op(P.dve, lambda h_, ps=ps, dst=dst, n_=n_: h_.reduce_max(out=dst, in_=ps[:, 0:n_], axis=AX.X), reads=[ps], writes=[kmp[mp]])
                P.op(P.dve, lambda h_, mp=mp, nch=nch: h_.reduce_max(out=km[mp][:, 0:1], in_=kmp[mp][:, 0:nch], axis=AX.X),
                     reads=[kmp[mp]], writes=[km[mp]])
            for g0 in range(0, nq, 512):
                n = min(512, nq - g0)
                qt = qrot.next()
                negc = negc_rot.next()
                for mp in range(nmaps):
                    for (ap, r0, nr) in qparts(h, mp):
                        P.dma(P.sp, qt[mp][r0:r0 + nr, 0:n], ap[:, q0 + g0:q0 + g0 + n], writes=[qt[mp]])
                for mp in range(nmaps):
                    sq = sqrot.next()
                    P.op(P.act, lambda h_, sq=sq, mp=mp: h_.activation(out=sq[0:d, 0:n], in_=qt[mp][0:d, 0:n], func=AF.Square),
                         reads=[qt[mp]], writes=[sq])
                    ps = srot.next() if diff else stat_ps
                    P.mm([lambda h_, ps=ps, sq=sq: h_.matmul(ps[:, 0:n], lhsT=ones_bf[0:d, :], rhs=sq[0:d, 0:n], start=True, stop=True)],
                         reads=[sq, ones_bf], writes=[ps])
                    nc_ = negc[mp]
                    P.op(P.dve, lambda h_, ps=ps, nc_=nc_: h_.reduce_max(out=nc_[:, 0:1], in_=ps[:, 0:n], axis=AX.X), reads=[ps], writes=[nc_])
                    P.op(P.dve, lambda h_, nc_=nc_, mp=mp: h_.tensor_tensor(out=nc_[:, 0:1], in0=nc_[:, 0:1], in1=km[mp][:, 0:1], op=ALU.mult),
                         reads=[nc_, km[mp]], writes=[nc_])
                    P.op(P.act, lambda h_, nc_=nc_: h_.activation(out=nc_[:, 0:1], in_=nc_[:, 0:1], func=AF.Sqrt), reads=[nc_], writes=[nc_])
                    P.op(P.dve, lambda h_, nc_=nc_: h_.tensor_scalar_mul(out=nc_[:, 0:1], in0=nc_[:, 0:1], scalar1=-scale), reads=[nc_], writes=[nc_])
                acc = accrot.next()
                for kti in range(nkt):
                    pts = []
                    for mp in range(nmaps):
                        ps = srot.next()
                        P.mm([lambda h_, ps=ps, mp=mp, kti=kti: h_.matmul(ps[:, 0:n], lhsT=kt[mp][0:d, kti * 128:(kti + 1) * 128], rhs=qt[mp][0:d, 0:n],
                                                                        start=True, stop=True)],
                             reads=[kt[mp], qt[mp]], writes=[ps])
                        pt = prot.next()
                        P.op(P.act, lambda h_, ps=ps, pt=pt, mp=mp: h_.activation(out=pt[:, 0:n], in_=ps[:, 0:n], func=AF.Exp,
                                                                                 bias=negc[mp][:, 0:1], scale=scale),
                             reads=[ps, negc[mp]], writes=[pt])
                        pts.append(pt)
                    first, last = (kti == 0), (kti == nkt - 1)
                    if not diff:
                        P.mm([lambda h_, kti=kti, pt=pts[0], first=first, last=last: h_.matmul(acc[0][:, 0:n], lhsT=vt[:, kti, :], rhs=pt[:, 0:n], start=first, stop=last)],
                             reads=[vt, pts[0]], writes=[acc[0]], acc=not first)
                    else:
                        for mp in range(2):
                            P.mm([lambda h_, kti=kti, pt=pts[mp], mp=mp, first=first, last=last: h_.matmul(acc[2 * mp][:, 0:n], lhsT=vt[:, kti, :], rhs=pt[:, 0:n], start=first, stop=last)],
                                 reads=[vt, pts[mp]], writes=[acc[2 * mp]], acc=not first)
                            P.mm([lambda h_, pt=pts[mp], mp=mp, first=first, last=last: h_.matmul(acc[2 * mp + 1][:, 0:n], lhsT=ones_bf[:, :], rhs=pt[:, 0:n], start=first, stop=last)],
                                 reads=[ones_bf, pts[mp]], writes=[acc[2 * mp + 1]], acc=not first)
                st = strot.next()
                orow = obase + h * ow
                if not diff:
                    rs = f32rot.next()
                    po = acc[0]
                    P.op(P.act, lambda h_, rs=rs, po=po: h_.copy(out=rs[0:64, 0:n], in_=po[64:128, 0:n]), reads=[po], writes=[rs])
                    P.op(P.dve, lambda h_, rs=rs: h_.reciprocal(out=rs[0:64, 0:n], in_=rs[0:64, 0:n]), reads=[rs], writes=[rs])
                    P.op(P.dve, lambda h_, rs=rs, po=po, st=st: h_.tensor_tensor(out=st[0:64, 0:n], in0=po[0:64, 0:n], in1=rs[0:64, 0:n], op=ALU.mult),
                         reads=[po, rs], writes=[st])
                    P.dma(P.sp, S["oT"][orow:orow + 64, q0 + g0:q0 + g0 + n], st[0:64, 0:n], reads=[st])
                else:
                    r1, a_, r2, b_, o_ = [f32rot.next() for _ in range(5)]
                    for (po, psm, r, out_) in ((acc[0], acc[1], r1, a_), (acc[2], acc[3], r2, b_)):
                        P.op(P.dve, lambda h_, r=r, psm=psm: h_.reciprocal(out=r[:, 0:n], in_=psm[:, 0:n]), reads=[psm], writes=[r])
                        P.op(P.dve, lambda h_, r=r, po=po, out_=out_: h_.tensor_tensor(out=out_[:, 0:n], in0=po[:, 0:n], in1=r[:, 0:n], op=ALU.mult),
                             reads=[po, r], writes=[out_])
                    P.op(P.dve, lambda h_: h_.scalar_tensor_tensor(out=o_[:, 0:n], in0=b_[:, 0:n], scalar=neglam[:, 0:1], in1=a_[:, 0:n],
                                                                  op0=ALU.mult, op1=ALU.add), reads=[a_, b_, neglam], writes=[o_])
                    sq = sqrot.next()
                    P.op(P.act, lambda h_, sq=sq: h_.activation(out=sq[:, 0:n], in_=o_[:, 0:n], func=AF.Square), reads=[o_], writes=[sq])
                    ps = srot.next()
                    P.mm([lambda h_, ps=ps, sq=sq: h_.matmul(ps[:, 0:n], lhsT=ones_bf[:, :], rhs=sq[:, 0:n], start=True, stop=True)],
                         reads=[sq, ones_bf], writes=[ps])
                    P.op(P.act, lambda h_, ps=ps: h_.activation(out=r1[:, 0:n], in_=ps[:, 0:n], func=AF.Sqrt, bias=eps_t[:, 0:1], scale=1.0 / 128),
                         reads=[ps, eps_t], writes=[r1])
                    P.op(P.dve, lambda h_: h_.reciprocal(out=r1[:, 0:n], in_=r1[:, 0:n]), reads=[r1], writes=[r1])
                    P.op(P.dve, lambda h_, st=st: h_.scalar_tensor_tensor(out=st[:, 0:n], in0=o_[:, 0:n], scalar=gsc[:, 0:1], in1=r1[:, 0:n],
                                                                         op0=ALU.mult, op1=ALU.mult), reads=[o_, r1, gsc], writes=[st])
                    P.dma(P.sp, S["oT"][orow:orow + 128, q0 + g0:q0 + g0 + n], st[:, 0:n], reads=[st])
    P.barrier()
    P.release(m)


def phase_lam(ctx, l):
    P, I = ctx["P"], ctx["I"]
    neglam, gsc = ctx["neglam"], ctx["dgsc"]
    m = P.mark()
    dl = P.sbuf("dl", [128, 256], F32)
    pr = P.sbuf("dlp", [128, 128], F32)
    e = P.sbuf("dle", [128, 2], F32)
    P.dma(P.sp, dl[:, :], I["dlam"][l:l + 1, :].to_broadcast([128, 256]), writes=[dl])
    for i in range(2):
        P.op(P.dve, lambda h, i=i: h.tensor_tensor(out=pr[:, i * 64:(i + 1) * 64], in0=dl[:, i * 128:i * 128 + 64], in1=dl[:, i * 128 + 64:i * 128 + 128], op=ALU.mult),
             reads=[dl], writes=[pr])
        P.op(P.dve, lambda h, i=i: h.reduce_sum(out=e[:, i:i + 1], in_=pr[:, i * 64:(i + 1) * 64], axis=AX.X), reads=[pr], writes=[e])
    P.op(P.act, lambda h: h.activation(out=e[:, :], in_=e[:, :], func=AF.Exp), reads=[e], writes=[e])
    P.op(P.dve, lambda h: h.tensor_scalar_add(out=e[:, 1:2], in0=e[:, 1:2], scalar1=-lam_init_of(l)), reads=[e], writes=[e])
    P.op(P.dve, lambda h: h.tensor_tensor(out=neglam[:, 0:1], in0=e[:, 1:2], in1=e[:, 0:1], op=ALU.subtract), reads=[e], writes=[neglam])
    P.dma(P.sp, gsc[:, 0:1], I["dng"][l].rearrange("(p o) -> p o", o=1), writes=[gsc])
    P.op(P.dve, lambda h: h.tensor_scalar_mul(out=gsc[:, 0:1], in0=gsc[:, 0:1], scalar1=1.0 - lam_init_of(l)), reads=[gsc], writes=[gsc])
    P.barrier()
    P.release(m)


def na_classes(rows):
    cls_list, cls_of_row, kb_of_row = [], [], []
    keys = {}
    for r in range(rows):
        ws = min(max(r - 4, 0), rows - 8)
        kb = min(max(((r - 7) // 2) * 2, 0), rows - 16)
        key = (kb - r, ws - r)
        if key not in keys:
            halves = []
            for half in range(2):
                js = [j for j in range(8) if ws <= kb + 2 * j + half < ws + 8]
                if js:
                    halves.append((half, js[0], len(js), kb + 2 * js[0] + half - r + 7))
            keys[key] = len(cls_list)
            cls_list.append(halves)
        cls_of_row.append(keys[key])
        kb_of_row.append(kb)
    return cls_list, cls_of_row, kb_of_row


def phase_na_s(ctx, l):
    P, cfg, I, S, PS, ones_bf = ctx["P"], ctx["cfg"], ctx["I"], ctx["S"], ctx["PS"], ctx["ones_bf"]
    TS, TKS, rows = cfg.ts, cfg.tks, cfg.rows
    scale = 64 ** -0.5
    d = 64
    cls_list, cls_of_row, kb_of_row = na_classes(rows)
    ncls = len(cls_list)
    m = P.mark()
    cvt = P.sbuf("cvt", [128, 15, 64], F32)
    P.dma(P.sp, cvt[:, :, :], I["cvt"], writes=[cvt])
    erot = Rot([P.sbuf("E%d" % i, [128, 15, 64], F32) for i in range(2)])
    ebrot = Rot([P.sbuf("EB%d" % i, [128, ncls, 8, 64], BF16) for i in range(2)])
    krot = Rot([P.sbuf("nkT%d" % i, [128, TKS], BF16) for i in range(2)])
    vrot = Rot([P.sbuf("nvt%d" % i, [128, TKS // 128, 128], BF16) for i in range(2)])
    qrot = Rot([P.sbuf("nqT%d" % i, [128, TS], BF16) for i in range(2)])
    prot = Rot([P.sbuf("npT%d" % i, [128, 512], BF16) for i in range(4)])
    pwrot = Rot([P.sbuf("npw%d" % i, [128, 512], F32) for i in range(3)])
    sqrot = Rot([P.sbuf("nsq%d" % i, [128, 512], BF16) for i in range(2)])
    strot = Rot([P.sbuf("nst%d" % i, [128, 512], BF16) for i in range(3)])
    rsrot = Rot([P.sbuf("nrs%d" % i, [128, 512], F32) for i in range(2)])
    kmp = P.sbuf("nkmp", [128, 16], F32)
    km = P.sbuf("nkm", [128, 1], F32)
    negc_rot = Rot([P.sbuf("nnegc%d" % i, [128, 1], F32) for i in range(2)])
    srot = Rot(PS[0:4])
    accrot = Rot([PS[4], PS[5]])
    stat_ps = PS[6]
    rp = I["rpbT"][l]
    for h in range(8):
        E = erot.next()
        for half in range(2):
            P.dma(P.sp, E[half * 64:(half + 1) * 64, :, :], rp[h].rearrange("r k q -> k r q"), writes=[E])
        P.op(P.act, lambda h_: h_.activation(out=E[:, :, :], in_=E[:, :, :], func=AF.Exp), reads=[E], writes=[E])
        P.op(P.dve, lambda h_: h_.tensor_tensor(out=E[:, :, :], in0=E[:, :, :], in1=cvt[:, :, :], op=ALU.mult), reads=[E, cvt], writes=[E])
        EB = ebrot.next()
        P.op(P.pool, lambda h_: h_.memset(EB[:, :, :, :], 0.0), writes=[EB])
        for ci, halves in enumerate(cls_list):
            for (half, j0, nj, dr0) in halves:
                P.op(P.dve, lambda h_, ci=ci, half=half, j0=j0, nj=nj, dr0=dr0: h_.tensor_copy(
                    out=EB[half * 64:(half + 1) * 64, ci, j0:j0 + nj, :],
                    in_=E[half * 64:(half + 1) * 64, dr0:dr0 + 2 * nj - 1:2, :]), reads=[E], writes=[EB])
        kt, vt, qt = krot.next(), vrot.next(), qrot.next()
        P.dma(P.sp, kt[0:64, :], S["nakT"][h * 64:(h + 1) * 64, 0:TKS], writes=[kt])
        for c8 in range(0, TKS // 128, 8):
            e8 = min(TKS // 128, c8 + 8)
            P.dma(P.sp, vt[:, c8:e8, :], S["navA"][c8 * 128:e8 * 128, h * 128:(h + 1) * 128].rearrange("(t p) j -> p t j", p=128), writes=[vt])
        P.dma(P.sp, qt[0:64, :], S["naqT"][h * 64:(h + 1) * 64, 0:TS], writes=[qt])
        nch = TKS // 512
        for ci in range(nch):
            sq = sqrot.next()
            P.op(P.act, lambda h_, sq=sq, ci=ci: h_.activation(out=sq[0:d, :], in_=kt[0:d, ci * 512:(ci + 1) * 512], func=AF.Square), reads=[kt], writes=[sq])
            P.mm([lambda h_, sq=sq: h_.matmul(stat_ps[:, :], lhsT=ones_bf[0:d, :], rhs=sq[0:d, :], start=True, stop=True)],
                 reads=[sq, ones_bf], writes=[stat_ps])
            P.op(P.dve, lambda h_, ci=ci: h_.reduce_max(out=kmp[:, ci:ci + 1], in_=stat_ps[:, :], axis=AX.X), reads=[stat_ps], writes=[kmp])
        P.op(P.dve, lambda h_: h_.reduce_max(out=km[:, 0:1], in_=kmp[:, 0:nch], axis=AX.X), reads=[kmp], writes=[km])
        for g0 in range(0, TS, 512):
            n = 512
            negc = negc_rot.next()
            sq = sqrot.next()
            P.op(P.act, lambda h_, sq=sq: h_.activation(out=sq[0:d, :], in_=qt[0:d, g0:g0 + n], func=AF.Square), reads=[qt], writes=[sq])
            P.mm([lambda h_, sq=sq: h_.matmul(stat_ps[:, :], lhsT=ones_bf[0:d, :], rhs=sq[0:d, :], start=True, stop=True)],
                 reads=[sq, ones_bf], writes=[stat_ps])
            P.op(P.dve, lambda h_: h_.reduce_max(out=negc[:, 0:1], in_=stat_ps[:, :], axis=AX.X), reads=[stat_ps], writes=[negc])
            P.op(P.dve, lambda h_: h_.tensor_tensor(out=negc[:, 0:1], in0=negc[:, 0:1], in1=km[:, 0:1], op=ALU.mult), reads=[negc, km], writes=[negc])
            P.op(P.act, lambda h_: h_.activation(out=negc[:, 0:1], in_=negc[:, 0:1], func=AF.Sqrt), reads=[negc], writes=[negc])
            P.op(P.dve, lambda h_: h_.tensor_scalar_mul(out=negc[:, 0:1], in0=negc[:, 0:1], scalar1=-scale), reads=[negc], writes=[negc])
            acc = accrot.next()
            for kti in range(PAST // 128):
                ps = srot.next()
                kc0 = TS + kti * 128
                P.mm([lambda h_, ps=ps, kc0=kc0: h_.matmul(ps[:, 0:n], lhsT=kt[0:d, kc0:kc0 + 128], rhs=qt[0:d, g0:g0 + n], start=True, stop=True)],
                     reads=[kt, qt], writes=[ps])
                pt = prot.next()
                P.op(P.act, lambda h_, ps=ps, pt=pt: h_.activation(out=pt[:, 0:n], in_=ps[:, 0:n], func=AF.Exp, bias=negc[:, 0:1], scale=scale),
                     reads=[ps, negc], writes=[pt])
                P.mm([lambda h_, pt=pt, kti=kti: h_.matmul(acc[:, 0:n], lhsT=vt[:, TS // 128 + kti, :], rhs=pt[:, 0:n], start=(kti == 0), stop=False)],
                     reads=[vt, pt], writes=[acc], acc=(kti > 0))
            for rr in range(8):
                r = g0 // 64 + rr
                kb = kb_of_row[r]
                ci = cls_of_row[r]
                q_lo = r * 64
                ps = srot.next()
                P.mm([lambda h_, ps=ps, j=j, kb=kb, q_lo=q_lo: h_.matmul(ps[:, j * 64:(j + 1) * 64], lhsT=kt[0:d, (kb + 2 * j) * 64:(kb + 2 * j) * 64 + 128],
                                                                       rhs=qt[0:d, q_lo:q_lo + 64], start=True, stop=True) for j in range(8)],
                     reads=[kt, qt], writes=[ps])
                pw = pwrot.next()
                P.op(P.act, lambda h_, ps=ps, pw=pw: h_.activation(out=pw[:, :], in_=ps[:, :], func=AF.Exp, bias=negc[:, 0:1], scale=scale),
                     reads=[ps, negc], writes=[pw])
                pt = prot.next()
                P.op(P.dve, lambda h_, pw=pw, pt=pt, ci=ci: h_.tensor_tensor(out=pt[:, :], in0=pw[:, :], in1=EB[:, ci, :, :].rearrange("p j q -> p (j q)"), op=ALU.mult),
                     reads=[pw, EB], writes=[pt])
                P.mm([lambda h_, pt=pt, j=j, kb=kb, rr=rr: h_.matmul(acc[:, rr * 64:(rr + 1) * 64], lhsT=vt[:, kb // 2 + j, :], rhs=pt[:, j * 64:(j + 1) * 64],
                                                                   start=False, stop=(j == 7 and rr == 7)) for j in range(8)],
                     reads=[vt, pt], writes=[acc], acc=True)
            rs = rsrot.next()
            st = strot.next()
            P.op(P.act, lambda h_, rs=rs, acc=acc: h_.copy(out=rs[0:64, 0:n], in_=acc[64:128, 0:n]), reads=[acc], writes=[rs])
            P.op(P.dve, lambda h_, rs=rs: h_.reciprocal(out=rs[0:64, 0:n], in_=rs[0:64, 0:n]), reads=[rs], writes=[rs])
            P.op(P.dve, lambda h_, rs=rs, acc=acc, st=st: h_.tensor_tensor(out=st[0:64, 0:n], in0=acc[0:64, 0:n], in1=rs[0:64, 0:n], op=ALU.mult),
                 reads=[acc, rs], writes=[st])
            P.dma(P.sp, S["oT"][512 + h * 64:512 + (h + 1) * 64, g0:g0 + n], st[0:64, 0:n], reads=[st])
    P.barrier()
    P.release(m)


def phase_pool(ctx, l):
    P, cfg, I, S, PS = ctx["P"], ctx["cfg"], ctx["I"], ctx["S"], ctx["PS"]
    TS = cfg.ts
    m = P.mark()
    nmax = max(TS, SEQ_P)
    M = 16
    ub = P.sbuf("pl_ub", [128, nmax], BF16)
    U = P.sbuf("pl_U", [128, nmax + 2 * M], F32)
    A = P.sbuf("pl_A", [128, nmax + 2 * M], F32)
    B = P.sbuf("pl_B", [128, nmax + 2 * M], F32)
    rc = P.sbuf("pl_rc", [128, nmax], F32)
    pb = P.sbuf("pl_pb", [128, nmax], BF16)
    pw = P.sbuf("pl_w", [128, 4, 128], BF16)
    psc = P.sbuf("pl_sc", [128, 4], F32)
    strot = Rot([P.sbuf("pl_st%d" % i, [128, 512], BF16) for i in range(3)])
    psrot = Rot(PS[0:4])
    P.dma(P.pool, pw[:, :, :], I["pool_w"][l].rearrange("g c d -> c g d"), writes=[pw])
    P.dma(P.sp, psc[:, :], I["pool_scale"][l].rearrange("(g p) -> p g", p=128), writes=[psc], allow_slow_non_contiguous=True)
    seqs = [(0, TS, "rcS")] + [(TS + j * SEQ_P, SEQ_P, "rcP") for j in range(cfg.n_p)]
    for gi, w in enumerate((2, 4, 8, 16)):
        for (t0, n, rcn) in seqs:
            W_ = n + 2 * M
            P.dma(P.sp, ub[:, 0:n], S["poolT"][gi * 128:(gi + 1) * 128, t0:t0 + n], writes=[ub])
            P.dma(P.sp, rc[:, 0:n], I[rcn][gi:gi + 1, :].to_broadcast([128, n]), writes=[rc])
            P.op(P.pool, lambda h: h.memset(U[:, 0:W_], 0.0), writes=[U])
            P.op(P.dve, lambda h: h.tensor_copy(out=U[:, M:M + n], in_=ub[:, 0:n]), reads=[ub], writes=[U])
            src, dst = U, A
            P.op(P.dve, lambda h: h.tensor_tensor(out=A[:, 1:W_], in0=U[:, 0:W_ - 1], in1=U[:, 1:W_], op=ALU.add), reads=[U], writes=[A])
            cur, other = A, B
            lo, hi, sh = 1, W_, 1
            for lvl in range(int(math.log2(w)) - 1):
                nlo, nhi = lo + sh, hi - sh
                P.op(P.dve, lambda h, cur=cur, other=other, nlo=nlo, nhi=nhi, sh=sh: h.tensor_tensor(
                    out=other[:, nlo:nhi], in0=cur[:, nlo - sh:nhi - sh], in1=cur[:, nlo + sh:nhi + sh], op=ALU.add), reads=[cur], writes=[other])
                cur, other = other, cur
                lo, hi, sh = nlo, nhi, sh * 2
            P.op(P.dve, lambda h, cur=cur: h.tensor_tensor(out=cur[:, M:M + n], in0=cur[:, M:M + n], in1=rc[:, 0:n], op=ALU.mult), reads=[cur, rc], writes=[cur])
            P.op(P.dve, lambda h, cur=cur: h.tensor_tensor(out=pb[:, 0:n], in0=cur[:, M:M + n], in1=U[:, M:M + n], op=ALU.subtract), reads=[cur, U], writes=[pb])
            for c0 in range(0, n, 512):
                nn = min(512, n - c0)
                ps = psrot.next()
                P.mm([lambda h, ps=ps, c0=c0, nn=nn: h.matmul(ps[:, 0:nn], lhsT=pw[:, gi, :], rhs=pb[:, c0:c0 + nn], start=True, stop=True)],
                     reads=[pw, pb], writes=[ps])
                st = strot.next()
                P.op(P.act, lambda h, ps=ps, st=st, nn=nn: h.activation(out=st[:, 0:nn], in_=ps[:, 0:nn], func=AF.Copy, scale=psc[:, gi:gi + 1]),
                     reads=[ps, psc], writes=[st])
                P.dma(P.sp, S["oT"][1024 + gi * 128:1024 + (gi + 1) * 128, t0 + c0:t0 + c0 + nn], st[:, 0:nn], reads=[st])
    P.barrier()
    P.release(m)


def phase_merge(ctx, l):
    P, cfg, I, S, PS = ctx["P"], ctx["cfg"], ctx["I"], ctx["S"], ctx["PS"]
    m = P.mark()
    wt = P.sbuf("wbr", [128, 16, D], BF16)
    P.dma(P.pool, wt[:, :, :], I["w_br"][l].rearrange("b (c p) n -> p (b c) n", p=128), writes=[wt])
    orot = Rot([P.sbuf("mo%d" % i, [128, 16, 512], BF16) for i in range(2)])
    grot = Rot([P.sbuf("mg%d" % i, [128, 4, 512], BF16) for i in range(3)])
    mrot = Rot([P.sbuf("mm%d" % i, [128, 16, 512], BF16) for i in range(2)])
    crot = Rot([P.sbuf("mc%d" % i, [128, 512], F32) for i in range(4)])
    trot = Rot([P.sbuf("mt%d" % i, [128, 512], F32) for i in range(8)])
    psrot = Rot(PS[0:8])
    oT_v = S["oT"].rearrange("(c p) t -> p c t", p=128)
    mT_v = S["mT"].rearrange("(c p) t -> p c t", p=128)
    gT_v = S["gT"].rearrange("(b f p) t -> p b f t", b=4, p=128)
    for (t0, n, c, _) in cfg.groups:
        o = orot.next()
        P.dma(P.sp, o[:, :, 0:n], oT_v[:, :, t0:t0 + n], writes=[o])
        mg = mrot.next()
        for f in range(16):
            g = grot.next()
            P.dma(P.sp, g[:, :, 0:n], gT_v[:, :, f, t0:t0 + n], writes=[g])
            pss = []
            for b in range(4):
                ps = psrot.next()
                P.mm([lambda h, ps=ps, b=b, kc=kc, f=f, o=o: h.matmul(ps[:, 0:n], lhsT=wt[:, b * 4 + kc, f * 128:(f + 1) * 128], rhs=o[:, b * 4 + kc, 0:n],
                                                                   start=(kc == 0), stop=(kc == 3)) for kc in range(4)],
                     reads=[wt, o], writes=[ps])
                pss.append(ps)
            t0_, t1_, s01, t2_, t3_, s23 = [trot.next() for _ in range(6)]
            c2, c3 = crot.next(), crot.next()
            P.op(P.act, lambda h, c2=c2: h.copy(out=c2[:, 0:n], in_=pss[2][:, 0:n]), reads=[pss[2]], writes=[c2])
            P.op(P.act, lambda h, c3=c3: h.copy(out=c3[:, 0:n], in_=pss[3][:, 0:n]), reads=[pss[3]], writes=[c3])
            P.op(P.dve, lambda h, g=g: h.tensor_tensor(out=t0_[:, 0:n], in0=pss[0][:, 0:n], in1=g[:, 0, 0:n], op=ALU.mult), reads=[pss[0], g], writes=[t0_])
            P.op(P.dve, lambda h, g=g: h.tensor_tensor(out=t1_[:, 0:n], in0=pss[1][:, 0:n], in1=g[:, 1, 0:n], op=ALU.mult), reads=[pss[1], g], writes=[t1_])
            P.op(P.pool, lambda h, g=g: h.tensor_tensor(out=t2_[:, 0:n], in0=c2[:, 0:n], in1=g[:, 2, 0:n], op=ALU.mult), reads=[c2, g], writes=[t2_])
            P.op(P.pool, lambda h, g=g: h.tensor_tensor(out=t3_[:, 0:n], in0=c3[:, 0:n], in1=g[:, 3, 0:n], op=ALU.mult), reads=[c3, g], writes=[t3_])
            P.op(P.dve, lambda h: h.tensor_tensor(out=s01[:, 0:n], in0=t0_[:, 0:n], in1=t1_[:, 0:n], op=ALU.add), reads=[t0_, t1_], writes=[s01])
            P.op(P.pool, lambda h: h.tensor_tensor(out=s23[:, 0:n], in0=t2_[:, 0:n], in1=t3_[:, 0:n], op=ALU.add), reads=[t2_, t3_], writes=[s23])
            P.op(P.dve, lambda h, f=f, mg=mg: h.tensor_tensor(out=mg[:, f, 0:n], in0=s01[:, 0:n], in1=s23[:, 0:n], op=ALU.add), reads=[s01, s23], writes=[mg])
        P.dma(P.sp, mT_v[:, :, t0:t0 + n], mg[:, :, 0:n], reads=[mg])
    P.barrier()
    P.release(m)


class RNBufs:
    def __init__(self, ctx):
        P = ctx["P"]
        self.xg = P.sbuf("rn_x", [128, 16, 512], F32)
        self.hg = P.sbuf("rn_h", [128, 16, 512], BF16)
        self.sq_rot = Rot([P.sbuf("rn_sq%d" % i, [128, 512], BF16) for i in range(3)])
        self.tmp_rot = Rot([P.sbuf("rn_t%d" % i, [128, 512], F32) for i in range(3)])
        self.rstd = P.sbuf("rn_r", [128, 512], F32)
        self.rstd2 = P.sbuf("rn_r2", [128, 512], F32)


def rn_update(ctx, rb, y, grp, G, AB, ss_ps, ss_ps2, ss_ready=False):
    P, S = ctx["P"], ctx["S"]
    t0, n, c, _ = grp
    xT_v = S["xT"].rearrange("(c p) t -> p c t", p=128)
    hT_v = S["hT"].rearrange("(c p) t -> p c t", p=128)
    xg = rb.xg
    P.dma(P.sp, xg[:, :, 0:n], xT_v[:, :, t0:t0 + n], writes=[xg])
    if not ss_ready:
        rstd_from_chunks(ctx, [y[:, fc, 0:n] for fc in range(16)], D, n, ss_ps, rb.rstd, rb.sq_rot, [y])
    else:
        P.op(P.act, lambda h: h.activation(out=rb.rstd[:, 0:n], in_=ss_ps[:, 0:n], func=AF.Sqrt, bias=ctx["eps_t"][:, 0:1], scale=1.0 / D),
             reads=[ss_ps, ctx["eps_t"]], writes=[rb.rstd])
        P.op(P.dve, lambda h: h.reciprocal(out=rb.rstd[:, 0:n], in_=rb.rstd[:, 0:n]), reads=[rb.rstd], writes=[rb.rstd])
    for fc in range(16):
        tmp = rb.tmp_rot.next()
        P.op(P.dve, lambda h, fc=fc, tmp=tmp: h.tensor_tensor(out=tmp[:, 0:n], in0=y[:, fc, 0:n], in1=rb.rstd[:, 0:n], op=ALU.mult),
             reads=[y, rb.rstd], writes=[tmp])
        P.op(P.dve, lambda h, fc=fc, tmp=tmp: h.scalar_tensor_tensor(out=xg[:, fc, 0:n], in0=tmp[:, 0:n], scalar=G[:, fc:fc + 1], in1=xg[:, fc, 0:n],
                                                                     op0=ALU.mult, op1=ALU.add), reads=[tmp, xg, ctx["modv"]], writes=[xg])
    P.dma(P.sp, xT_v[:, :, t0:t0 + n], xg[:, :, 0:n], reads=[xg])
    if AB is not None:
        mod_norm_group(ctx, xg, n, AB[0], AB[1], rb.hg, ss_ps2, rb.rstd2, rb.sq_rot, rb.tmp_rot)
        P.dma(P.sp, hT_v[:, :, t0:t0 + n], rb.hg[:, :, 0:n], reads=[rb.hg])


def phase_wo(ctx, l):
    P, cfg, I, S, PS, mv, ones_bf = ctx["P"], ctx["cfg"], ctx["I"], ctx["S"], ctx["PS"], ctx["mv"], ctx["ones_bf"]
    m = P.mark()
    wt = P.sbuf("wo", [128, 16, D], BF16)
    P.dma(P.pool, wt[:, :, :], I["w_o"][l].rearrange("(c p) n -> p c n", p=128), writes=[wt])
    mt = P.sbuf("wo_m", [128, 16, 512], BF16)
    y = P.sbuf("wo_y", [128, 16, 512], F32)
    rb = RNBufs(ctx)
    psrot = Rot(PS[0:5])
    mT_v = S["mT"].rearrange("(c p) t -> p c t", p=128)
    for grp in cfg.groups:
        t0, n, c, _ = grp
        P.dma(P.sp, mt[:, :, 0:n], mT_v[:, :, t0:t0 + n], writes=[mt])
        for f in range(16):
            ps = psrot.next()
            P.mm([lambda h, ps=ps, kc=kc, f=f: h.matmul(ps[:, 0:n], lhsT=wt[:, kc, f * 128:(f + 1) * 128], rhs=mt[:, kc, 0:n],
                                                      start=(kc == 0), stop=(kc == 15)) for kc in range(16)],
                 reads=[wt, mt], writes=[ps])
            P.evac(y[:, f, 0:n], ps[:, 0:n], [ps], [y])
        rn_update(ctx, rb, y, grp, mv(l, c, 2), (mv(l, c, 3), mv(l, c, 4)), PS[6], PS[7])
    P.barrier()
    P.release(m)


def phase_mlp(ctx, l):
    P, cfg, I, S, PS, mv = ctx["P"], ctx["cfg"], ctx["I"], ctx["S"], ctx["PS"], ctx["mv"]
    L = cfg.depth
    m = P.mark()
    st_rot = Rot([P.sbuf("us%d" % i, [128, 512], BF16) for i in range(4)])
    r_rot = Rot([P.sbuf("ur%d" % i, [128, 512], F32) for i in range(3)])

    def epi_u(ti, ps, mm_, g):
        t0, n = g[0], g[1]
        r = r_rot.next()
        P.op(P.dve, lambda h: h.tensor_scalar_max(out=r[:, 0:n], in0=ps[:, 0:n], scalar1=0.0), reads=[ps], writes=[r])
        st = st_rot.next()
        P.op(P.act, lambda h: h.activation(out=st[:, 0:n], in_=r[:, 0:n], func=AF.Square), reads=[r], writes=[st])
        P.dma(P.sp, S["uT"][ti * 128:(ti + 1) * 128, t0:t0 + n], st[:, 0:n], reads=[st])

    gemm_F(ctx, [(I["w_up"][l], 128)], D, S["hT"], cfg.groups, epi_u, wmax=2048)
    P.barrier()
    P.release(m)

    m = P.mark()
    f_rot = Rot([P.sbuf("df%d" % i, [128, 512], F32) for i in range(4)])

    def epi_d(ti, ps, mm_, g):
        t0, n = g[0], g[1]
        sf = f_rot.next()
        P.evac(sf[:, 0:n], ps[:, 0:n], [ps], [sf])
        P.dma(P.sp, S["y2T"][ti * 128:(ti + 1) * 128, t0:t0 + n], sf[:, 0:n], reads=[sf])

    gemm_F(ctx, [(I["w_down"][l], 128)], DFF, S["uT"], cfg.groups, epi_d, wmax=512, kpiece=16, ps_list=PS[0:8])
    P.barrier()
    P.release(m)

    m = P.mark()
    rb = RNBufs(ctx)
    yrot = Rot([P.sbuf("y2_%d" % i, [128, 16, 512], F32) for i in range(2)])
    y2_v = S["y2T"].rearrange("(c p) t -> p c t", p=128)
    for grp in cfg.groups:
        t0, n, c, _ = grp
        y = yrot.next()
        P.dma(P.sp, y[:, :, 0:n], y2_v[:, :, t0:t0 + n], writes=[y])
        AB = (mv(l + 1, c, 0), mv(l + 1, c, 1)) if l + 1 < L else None
        rn_update(ctx, rb, y, grp, mv(l, c, 5), AB, PS[6], PS[7])
    P.barrier()
    P.release(m)


def phase_out(ctx):
    P, cfg, O, S, PS, ident = ctx["P"], ctx["cfg"], ctx["O"], ctx["S"], ctx["PS"], ctx["ident"]
    m = P.mark()
    xrot = Rot([P.sbuf("ox%d" % i, [128, 16, 512], F32) for i in range(2)])
    yrot = Rot([P.sbuf("oy%d" % i, [128, D], F32) for i in range(3)])
    psrot = Rot(PS[0:6])
    xT_v = S["xT"].rearrange("(c p) t -> p c t", p=128)
    for (t0, n, c, _) in cfg.groups:
        xg = xrot.next()
        P.dma(P.sp, xg[:, :, 0:n], xT_v[:, :, t0:t0 + n], writes=[xg])
        for tt in range(n // 128):
            yt = yrot.next()
            for f4 in range(4):
                ps = psrot.next()
                for j in range(4):
                    fc = f4 * 4 + j
                    P.mm([lambda h, ps=ps, fc=fc, j=j, tt=tt: h.transpose(out=ps[:, j * 128:(j + 1) * 128], in_=xg[:, fc, tt * 128:(tt + 1) * 128], identity=ident[:, :])],
                         reads=[xg, ident], writes=[ps], acc=(j > 0))
                P.evac(yt[:, f4 * 512:(f4 + 1) * 512], ps[:, :], [ps], [yt])
            if c == 0:
                dst = O["ys"][t0 + tt * 128:t0 + (tt + 1) * 128, :]
            else:
                r0 = t0 - cfg.ts + tt * 128
                dst = O["yp"][r0:r0 + 128, :]
            P.dma(P.sp, dst, yt[:, :], reads=[yt])
    P.barrier()
    P.release(m)


def host_constants(cfg):
    TS = cfg.ts
    t = np.arange(TS)
    row = (t // GRID_W).astype(np.float32)
    col = (t % GRID_W).astype(np.float32)

    def rope(rot_dim):
        nf = rot_dim // 4
        inv = (10000.0 ** (-np.arange(nf, dtype=np.float32) / nf)).astype(np.float32)
        ang = np.concatenate([row[:, None] * inv, col[:, None] * inv], axis=-1).astype(np.float32)
        return np.cos(ang).astype(np.float32).T, np.sin(ang).astype(np.float32).T

    cm, sm = rope(32)
    cd, sd = rope(64)
    ropeM = np.stack([np.tile(cm, (8, 1)), np.tile(sm, (8, 1))]).astype(np.float32)
    ropeD = np.stack([np.tile(cd, (4, 1)), np.tile(sd, (4, 1))]).astype(np.float32)

    def rc(tlen):
        pos = np.arange(tlen)
        out = []
        for w in (2, 4, 8, 16):
            lo = np.clip(pos - w // 2, 0, tlen)
            hi = np.clip(pos + w // 2, 0, tlen)
            out.append(1.0 / (hi - lo).astype(np.float32))
        return np.stack(out).astype(np.float32)

    kc = np.arange(64)[:, None]
    qc = np.arange(64)[None, :]
    cs = np.clip(qc - 8, 0, 48)
    cv = ((kc >= cs) & (kc < cs + 16)).astype(np.float32)
    cvt = np.tile(cv[None, None], (2, 15, 1, 1)).transpose(0, 2, 1, 3).reshape(128, 15, 64)
    return dict(ident=np.eye(128, dtype=np.float32), ropeM=ropeM, ropeD=ropeD, rcS=rc(TS), rcP=rc(SEQ_P),
                cvt=np.ascontiguousarray(cvt))


def make_in_maps(cfg, inputs, n_cores, samples_of_core, prompts_of_core):
    consts = host_constants(cfg)
    f = lambda a: np.ascontiguousarray(np.asarray(a, dtype=np.float32))
    rpb = f(inputs["na_rpb"])
    kc = np.arange(64)[:, None]
    qc = np.arange(64)[None, :]
    dc = np.clip(kc - qc + 15, 0, 30)
    rpbT = np.ascontiguousarray(rpb[:, :, :, dc])
    shared = dict(
        w_mod=f(inputs["w_mod"]), b_mod=f(inputs["b_mod"]), g_norm=f(inputs["g_norm"]), w_in=f(inputs["w_in"]),
        g_q=f(inputs["g_q_lora"]), g_kv=f(inputs["g_kv_lora"]), w_uq=f(inputs["w_uq"]), w_ukv=f(inputs["w_ukv"]),
        rpbT=rpbT, pool_w=f(inputs["pool_w"]), pool_scale=f(inputs["pool_scale"]),
        dlam=f(inputs["diff_lambda"]).reshape(cfg.depth, 256), dng=f(inputs["diff_norm_g"]),
        w_br=f(inputs["w_br"]), w_o=f(inputs["w_o"]), w_up=f(inputs["w_up"]), w_down=f(inputs["w_down"]))
    shared.update(consts)
    maps = []
    for c in range(n_cores):
        b = samples_of_core[c]
        ps = prompts_of_core[c]
        m = dict(shared)
        m["xs"] = f(inputs["x_sample"][b])
        m["xp"] = f(np.asarray(inputs["x_prompt"])[ps].reshape(cfg.tp, D))
        m["cond"] = f(np.stack([np.asarray(inputs["c"])[b], np.asarray(inputs["c_ctx"])]))
        m["c_ckv"] = f(inputs["cache_mla_ckv"][b])
        m["c_kr"] = f(inputs["cache_mla_krope"][b])
        m["c_nak"] = f(np.asarray(inputs["cache_na_k"][b]).reshape(cfg.depth, PAST, 512))
        m["c_nav"] = f(np.asarray(inputs["cache_na_v"][b]).reshape(cfg.depth, PAST, 512))
        m["c_dk"] = f(np.asarray(inputs["cache_diff_k"][b]).reshape(cfg.depth, PAST, 512))
        m["c_dv"] = f(np.asarray(inputs["cache_diff_v"][b]).reshape(cfg.depth, PAST, 512))
        maps.append(m)
    return maps


def kernel(**inputs):
    cfg = Cfg()
    n = 8
    nc = build_program(cfg)
    samples = [c // 2 for c in range(n)]
    prompts = [list(range(4 * c, 4 * c + 4)) for c in range(n)]
    maps = make_in_maps(cfg, inputs, n, samples, prompts)
    res = run_bass_kernel_spmd(nc, maps, core_ids=list(range(n)))
    R = res.results
    y_prompt = np.concatenate([R[c]["yp"].reshape(4, SEQ_P, D) for c in range(n)], axis=0)
    y_sample = np.stack([R[2 * b]["ys"] for b in range(4)], axis=0)

    def st(name, shape_tail):
        return np.concatenate([R[c][name] for c in range(n)], axis=0).reshape((32, cfg.depth, SEQ_P) + shape_tail)

    return (y_prompt.astype(np.float32), y_sample.astype(np.float32),
            st("st_ckv", (256,)), st("st_kr", (32,)), st("st_nak", (8, 64)), st("st_nav", (8, 64)),
            st("st_dk", (4, 128)), st("st_dv", (4, 128)))
```
